# Optimizing a Trainium2 kernel written in Bass

```python
import math
import jax, jax.numpy as jnp
from jax import lax
import numpy as np

D_MODEL = 1024
BATCH = 8
SEQ = 2048
DEPTH = 1

PLE_DIM = 256
DA_HEADS = 8
DA_HEAD_DIM = 64
DA_V_DIM = 2 * DA_HEAD_DIM
DA_QK_WIDTH = DA_HEADS * 2 * DA_HEAD_DIM
DA_V_WIDTH = DA_HEADS * DA_V_DIM
GDN_QK_HEADS = 8
GDN_V_HEADS = 16
GDN_K_DIM = 128
GDN_V_DIM = 128
GDN_QK_WIDTH = GDN_QK_HEADS * GDN_K_DIM
GDN_V_WIDTH = GDN_V_HEADS * GDN_V_DIM
GDN_CONV_CH = 2 * GDN_QK_WIDTH + GDN_V_WIDTH
CONV_WIDTH = 4
CHUNK = 64
Q_BLOCK = 128
D_FF = 4 * D_MODEL
EPS = 1e-6
IN_SIZES = (DA_QK_WIDTH, DA_QK_WIDTH, DA_V_WIDTH, GDN_CONV_CH, GDN_V_WIDTH, GDN_V_HEADS, GDN_V_HEADS, D_MODEL, D_MODEL)
IN_WIDTH = sum(IN_SIZES)

kernel_name = 'hybrid_diffattn_gdn_block'


def _split_points():
    pts = []
    acc = 0
    for s in IN_SIZES[:-1]:
        acc += s
        pts.append(acc)
    return pts


def rmsnorm(x, gain):
    x32 = x.astype(jnp.float32)
    y = x32 * lax.rsqrt(jnp.mean(x32 * x32, axis=-1, keepdims=True) + EPS)
    return y.astype(x.dtype) * gain


def l2norm(x):
    x32 = x.astype(jnp.float32)
    return x32 * lax.rsqrt(jnp.sum(x32 * x32, axis=-1, keepdims=True) + EPS)


def causal_depthwise_conv(x, w):
    k_w, c = w.shape
    return lax.conv_general_dilated(x, w[:, None, :].astype(x.dtype), window_strides=(1,),
                                    padding=[(k_w - 1, 0)], dimension_numbers=('NWC', 'WIO', 'NWC'),
                                    feature_group_count=c)


def diff_attention(q, k, v, lq1, lk1, lq2, lk2, sub_gain, lambda_init):
    b, t = q.shape[0], q.shape[1]
    nb = t // Q_BLOCK
    lam = (jnp.exp(jnp.sum(lq1.astype(jnp.float32) * lk1.astype(jnp.float32)))
           - jnp.exp(jnp.sum(lq2.astype(jnp.float32) * lk2.astype(jnp.float32))) + lambda_init)
    qb = (q * DA_HEAD_DIM ** -0.5).reshape(b, nb, Q_BLOCK, 2 * DA_HEADS, DA_HEAD_DIM).swapaxes(0, 1)
    kpos = jnp.arange(t)

    def one_block(args):
        q_blk, blk = args
        s = jnp.einsum('bqhd,bkhd->bhqk', q_blk, k).astype(jnp.float32)
        qpos = blk * Q_BLOCK + jnp.arange(Q_BLOCK)
        s = jnp.where(kpos[None, :] <= qpos[:, None], s, -jnp.inf)
        w = jax.nn.softmax(s, axis=-1).reshape(b, DA_HEADS, 2, Q_BLOCK, t)
        a = w[:, :, 0] - lam * w[:, :, 1]
        return jnp.einsum('bhqk,bkhd->bqhd', a.astype(v.dtype), v)

    o = lax.map(one_block, (qb, jnp.arange(nb)))
    o = o.swapaxes(0, 1).reshape(b, t, DA_HEADS, DA_V_DIM)
    o = rmsnorm(o, sub_gain) * (1.0 - lambda_init)
    return o.reshape(b, t, DA_V_WIDTH)


def gated_delta_rule(q, k, v, g, beta):
    b, t, h, dk = q.shape
    dv = v.shape[-1]
    n = t // CHUNK

    def to_chunks(z):
        z = z.reshape((b, n, CHUNK, h) + z.shape[3:])
        return jnp.moveaxis(z, 3, 1)

    q = to_chunks(q * dk ** -0.5)
    k = to_chunks(k)
    v = to_chunks(v)
    g = to_chunks(g)
    beta = to_chunks(beta)
    gc = jnp.cumsum(g, axis=-1)
    causal = jnp.tril(jnp.ones((CHUNK, CHUNK), dtype=bool))
    decay = jnp.exp(jnp.where(causal, gc[..., :, None] - gc[..., None, :], -jnp.inf))
    k_beta = k * beta[..., None]
    v_beta = v * beta[..., None]
    lower = jnp.tril(jnp.einsum('bhncd,bhnsd->bhncs', k_beta, k) * decay, -1)
    eye = jnp.eye(CHUNK, dtype=jnp.float32)
    tmat = lax.linalg.triangular_solve(eye + lower, jnp.broadcast_to(eye, lower.shape),
                                       left_side=True, lower=True, unit_diagonal=True)
    u = jnp.einsum('bhncs,bhnse->bhnce', tmat, v_beta)
    w = jnp.einsum('bhncs,bhnsd->bhncd', tmat, k_beta * jnp.exp(gc)[..., None])
    intra = jnp.einsum('bhncd,bhnsd->bhncs', q, k) * decay

    def step(state, xs):
        q_c, k_c, u_c, w_c, gc_c, intra_c = xs
        v_new = u_c - jnp.einsum('bhcd,bhde->bhce', w_c, state)
        o_c = (jnp.einsum('bhcd,bhde->bhce', q_c * jnp.exp(gc_c)[..., None], state)
               + jnp.einsum('bhcs,bhse->bhce', intra_c, v_new))
        g_last = gc_c[..., -1]
        k_dec = k_c * jnp.exp(g_last[..., None] - gc_c)[..., None]
        state = state * jnp.exp(g_last)[..., None, None] + jnp.einsum('bhcd,bhce->bhde', k_dec, v_new)
        return state, o_c

    xs = tuple(jnp.moveaxis(z, 2, 0) for z in (q, k, u, w, gc, intra))
    state0 = jnp.zeros((b, h, dk, dv), jnp.float32)
    _, o = lax.scan(step, state0, xs)
    return jnp.transpose(o, (1, 0, 3, 2, 4)).reshape(b, t, h, dv)


def setup_inputs(seed: int = 0) -> dict:
    key = jax.random.key(seed)
    ks = jax.random.split(key, 32)

    def dense(k, fan_in, shape):
        return jax.random.normal(k, shape, jnp.float32) * fan_in ** -0.5

    def gain(k, n):
        return 1.0 + 0.02 * jax.random.normal(k, (DEPTH, n), jnp.float32)

    x = jax.random.normal(ks[0], (BATCH, SEQ, D_MODEL), jnp.float32)
    p = jax.random.normal(ks[1], (DEPTH, BATCH, SEQ, PLE_DIM), jnp.float32)
    pre_mix_norm = gain(ks[2], D_MODEL)
    w_in = dense(ks[3], D_MODEL, (DEPTH, D_MODEL, IN_WIDTH))
    conv_w = dense(ks[4], CONV_WIDTH, (DEPTH, CONV_WIDTH, GDN_CONV_CH))
    lambda_q1 = 0.1 * jax.random.normal(ks[5], (DEPTH, DA_HEAD_DIM), jnp.float32)
    lambda_k1 = 0.1 * jax.random.normal(ks[6], (DEPTH, DA_HEAD_DIM), jnp.float32)
    lambda_q2 = 0.1 * jax.random.normal(ks[7], (DEPTH, DA_HEAD_DIM), jnp.float32)
    lambda_k2 = 0.1 * jax.random.normal(ks[8], (DEPTH, DA_HEAD_DIM), jnp.float32)
    da_sub_norm = gain(ks[9], DA_V_DIM)
    gdn_a_log = jnp.log(jax.random.uniform(ks[10], (DEPTH, GDN_V_HEADS), jnp.float32, 1.0, 16.0))
    dt = jnp.exp(jax.random.uniform(ks[11], (DEPTH, GDN_V_HEADS), jnp.float32,
                                    math.log(1e-3), math.log(0.1)))
    gdn_dt_bias = dt + jnp.log(-jnp.expm1(-dt))
    gdn_out_norm = gain(ks[12], GDN_V_DIM)
    w_branch_a = dense(ks[13], DA_V_WIDTH, (DEPTH, DA_V_WIDTH, D_MODEL))
    w_branch_b = dense(ks[14], GDN_V_WIDTH, (DEPTH, GDN_V_WIDTH, D_MODEL))
    w_out = dense(ks[15], D_MODEL, (DEPTH, D_MODEL, D_MODEL))
    post_mix_norm = gain(ks[16], D_MODEL)
    pre_mlp_norm = gain(ks[17], D_MODEL)
    w_up = dense(ks[18], D_MODEL, (DEPTH, D_MODEL, D_FF))
    w_down = dense(ks[19], D_FF, (DEPTH, D_FF, D_MODEL))
    post_mlp_norm = gain(ks[20], D_MODEL)
    w_ple = dense(ks[21], PLE_DIM, (DEPTH, PLE_DIM, D_MODEL))
    w_ple_gate = dense(ks[22], D_MODEL, (DEPTH, D_MODEL, D_MODEL))
    ple_norm = gain(ks[23], D_MODEL)
    return {'x': x, 'p': p, 'pre_mix_norm': pre_mix_norm, 'w_in': w_in, 'conv_w': conv_w,
            'lambda_q1': lambda_q1, 'lambda_k1': lambda_k1, 'lambda_q2': lambda_q2, 'lambda_k2': lambda_k2,
            'da_sub_norm': da_sub_norm, 'gdn_a_log': gdn_a_log, 'gdn_dt_bias': gdn_dt_bias,
            'gdn_out_norm': gdn_out_norm, 'w_branch_a': w_branch_a, 'w_branch_b': w_branch_b,
            'w_out': w_out, 'post_mix_norm': post_mix_norm, 'pre_mlp_norm': pre_mlp_norm,
            'w_up': w_up, 'w_down': w_down, 'post_mlp_norm': post_mlp_norm,
            'w_ple': w_ple, 'w_ple_gate': w_ple_gate, 'ple_norm': ple_norm}


def reference(x, p, pre_mix_norm, w_in, conv_w, lambda_q1, lambda_k1, lambda_q2, lambda_k2,
              da_sub_norm, gdn_a_log, gdn_dt_bias, gdn_out_norm, w_branch_a, w_branch_b, w_out,
              post_mix_norm, pre_mlp_norm, w_up, w_down, post_mlp_norm, w_ple, w_ple_gate, ple_norm):
    b, t, _ = x.shape
    splits = _split_points()
    rep = GDN_V_HEADS // GDN_QK_HEADS
    h = x
    for i in range(DEPTH):
        lambda_init = 0.8 - 0.6 * math.exp(-0.3 * i)
        u = rmsnorm(h, pre_mix_norm[i])
        proj = jnp.einsum('btd,de->bte', u, w_in[i])
        (da_q, da_k, da_v, gdn_qkv, gdn_z, gdn_b, gdn_a,
         gate_a_in, gate_b_in) = jnp.split(proj, splits, axis=-1)
        o_a = diff_attention(da_q.reshape(b, t, 2 * DA_HEADS, DA_HEAD_DIM),
                             da_k.reshape(b, t, 2 * DA_HEADS, DA_HEAD_DIM),
                             da_v.reshape(b, t, DA_HEADS, DA_V_DIM),
                             lambda_q1[i], lambda_k1[i], lambda_q2[i], lambda_k2[i],
                             da_sub_norm[i], lambda_init)
        qkv = jax.nn.silu(causal_depthwise_conv(gdn_qkv, conv_w[i]))
        g_q, g_k, g_v = jnp.split(qkv, [GDN_QK_WIDTH, 2 * GDN_QK_WIDTH], axis=-1)
        g_q = jnp.repeat(l2norm(g_q.reshape(b, t, GDN_QK_HEADS, GDN_K_DIM)), rep, axis=2)
        g_k = jnp.repeat(l2norm(g_k.reshape(b, t, GDN_QK_HEADS, GDN_K_DIM)), rep, axis=2)
        g_v = g_v.reshape(b, t, GDN_V_HEADS, GDN_V_DIM).astype(jnp.float32)
        beta = jax.nn.sigmoid(gdn_b.astype(jnp.float32))
        log_decay = -jnp.exp(gdn_a_log[i].astype(jnp.float32)) * jax.nn.softplus(
            gdn_a.astype(jnp.float32) + gdn_dt_bias[i].astype(jnp.float32))
        o_b = gated_delta_rule(g_q, g_k, g_v, log_decay, beta)
        o_b = rmsnorm(o_b, gdn_out_norm[i].astype(jnp.float32)) * jax.nn.silu(
            gdn_z.reshape(b, t, GDN_V_HEADS, GDN_V_DIM).astype(jnp.float32))
        o_b = o_b.reshape(b, t, GDN_V_WIDTH).astype(x.dtype)
        merged = (jax.nn.sigmoid(gate_a_in) * jnp.einsum('bte,ed->btd', o_a, w_branch_a[i])
                  + jax.nn.sigmoid(gate_b_in) * jnp.einsum('bte,ed->btd', o_b, w_branch_b[i]))
        mix = jnp.einsum('btd,de->bte', merged, w_out[i])
        h = h + rmsnorm(mix, post_mix_norm[i])
        u = rmsnorm(h, pre_mlp_norm[i])
        hid = jnp.square(jax.nn.relu(jnp.einsum('btd,df->btf', u, w_up[i])))
        h = h + rmsnorm(jnp.einsum('btf,fd->btd', hid, w_down[i]), post_mlp_norm[i])
        e = jnp.einsum('btp,pd->btd', p[i], w_ple[i]) * jax.nn.sigmoid(
            jnp.einsum('btd,de->bte', h, w_ple_gate[i]))
        h = h + rmsnorm(e, ple_norm[i])
    return h
```

```python
import numpy as np
import concourse.bass as bass
import concourse.mybir as mybir

F32 = mybir.dt.float32
BF16 = mybir.dt.bfloat16
AF = mybir.ActivationFunctionType
ALU = mybir.AluOpType
AX = mybir.AxisListType

ENGS = ("pe", "act", "dve", "pool", "sp")
NDMA = 8


def _prod(xs):
    r = 1
    for v in xs:
        r *= int(v)
    return r


class Op:
    __slots__ = ("eng", "fn", "deps", "inc", "cnt", "dma", "dsem", "dval", "idx")


class Sched:
    def __init__(self, nc):
        self.nc = nc
        self.streams = {e: [] for e in ENGS}
        self.acc = {}
        self.ndma = {e: 0 for e in ENGS}
        self.whole = set()

    def rect(self, ap):
        t = ap.tensor
        name = t.name
        es = mybir.dt.size(ap.dtype)
        space = str(ap.space)
        off = int(ap.offset)
        pat = ap.ap
        if "DRAM" in space.upper() or space.upper().startswith("D"):
            lo = off
            hi = off
            for st, c in pat:
                if st >= 0:
                    hi += st * (c - 1)
                else:
                    lo += st * (c - 1)
            return name, 0, 1, lo * es, (hi + 1) * es
        pstep = _prod(t.shape[1:])
        p0 = off // pstep
        f0 = off % pstep
        np_ = pat[0][1]
        f1 = f0
        for st, c in pat[1:]:
            f1 += st * (c - 1)
        if name in self.whole:
            return name, 0, 128, 0, 1 << 30
        return name, p0, p0 + np_, f0 * es, (f1 + 1) * es

    def add(self, eng, fn, reads=(), writes=(), dma=False):
        op = Op()
        op.eng = eng
        op.fn = fn
        op.inc = False
        op.cnt = None
        op.dma = dma
        op.deps = []
        op.idx = len(self.streams[eng])
        deps = {}
        for ap in reads:
            self._access(ap, op, False, deps)
        for ap in writes:
            self._access(ap, op, True, deps)
        best = {}
        for d in deps.values():
            key = ("d",) + d.dsem if d.dma else d.eng
            cur = best.get(key)
            if cur is None or (d.dval > cur.dval if d.dma else d.idx > cur.idx):
                best[key] = d
        op.deps = list(best.values())
        for d in op.deps:
            d.inc = True
        if dma:
            i = self.ndma[eng]
            self.ndma[eng] += 1
            op.dsem = (eng, i % NDMA)
            op.dval = 16 * (i // NDMA + 1)
        self.streams[eng].append(op)
        return op

    def _need(self, p, pw, o, ow):
        if not (pw or ow):
            return False
        if p is o:
            return False
        if p.dma or o.dma:
            return True
        if p.eng == o.eng:
            if o.eng == "pe":
                return False
            return pw and not ow
        return True

    def _access(self, ap, op, is_w, deps):
        name, p0, p1, b0, b1 = self.rect(ap)
        lst = self.acc.setdefault(name, [])
        keep = []
        for ent in lst:
            q0, q1, c0, c1, pop, pw = ent
            ov = not (q1 <= p0 or p1 <= q0 or c1 <= b0 or b1 <= c0)
            if ov and self._need(pop, pw, op, is_w):
                deps[id(pop)] = pop
            covered = q0 >= p0 and q1 <= p1 and c0 >= b0 and c1 <= b1
            if pop is op:
                keep.append(ent)
            elif is_w and covered:
                continue
            elif (not is_w) and (not pw) and covered and pop.eng == op.eng and not pop.dma and not op.dma:
                continue
            else:
                keep.append(ent)
        keep.append((p0, p1, b0, b1, op, is_w))
        self.acc[name] = keep

    def emit(self, block, sems, dsems, final_engine="sp"):
        nc = self.nc
        for e in ENGS:
            c = 0
            for op in self.streams[e]:
                if op.inc and not op.dma:
                    c += 1
                    op.cnt = c
        eng_obj = {"pe": "tensor", "act": "scalar", "dve": "vector", "pool": "gpsimd", "sp": "sync"}

        def run_stream(e, engine):
            seen = {}
            for op in self.streams[e]:
                for d in op.deps:
                    if d.dma:
                        s = dsems[d.dsem]
                        key = ("d",) + d.dsem
                        val = d.dval
                    else:
                        s = sems[d.eng]
                        key = d.eng
                        val = d.cnt
                    if seen.get(key, 0) < val:
                        engine.wait_ge(s, val)
                        seen[key] = val
                ins = op.fn(engine)
                if op.dma:
                    ins.then_inc(dsems[op.dsem], 16)
                elif op.inc:
                    ins.then_inc(sems[e], 1)
            if e == final_engine:
                for q in ENGS:
                    n = self.ndma[q]
                    for r in range(min(n, NDMA)):
                        cntr = (n - r + NDMA - 1) // NDMA
                        engine.wait_ge(dsems[(q, r)], 16 * cntr)

        for e in ENGS:
            getattr(block, eng_obj[e])(lambda engine, e=e: run_stream(e, engine))


class K:
    def __init__(self, nc):
        self.nc = nc
        self.s = Sched(nc)

    def dma(self, q, out, in_, **kw):
        return self.s.add(q, lambda e: e.dma_start(out=out, in_=in_, **kw), [in_], [out], dma=True)

    def mm(self, out, lhsT, rhs, start=True, stop=True):
        return self.s.add("pe", lambda e: e.matmul(out, lhsT, rhs, start=start, stop=stop), [lhsT, rhs], [out])

    def tr(self, out, in_, ident):
        return self.s.add("pe", lambda e: e.transpose(out, in_, ident), [in_, ident], [out])

    def act(self, out, in_, func, bias=None, scale=None, accum_out=None, eng="act"):
        kw = {}
        rd = [in_]
        wr = [out]
        if bias is not None:
            kw["bias"] = bias
            if not isinstance(bias, (int, float)):
                rd.append(bias)
        if scale is not None:
            kw["scale"] = scale
            if not isinstance(scale, (int, float)):
                rd.append(scale)
        if accum_out is not None:
            kw["accum_out"] = accum_out
            wr.append(accum_out)
        return self.s.add("act", lambda e: e.activation(out, in_, func, **kw), rd, wr)

    def tt(self, eng, out, in0, in1, op):
        return self.s.add(eng, lambda e: e.tensor_tensor(out, in0, in1, op), [in0, in1], [out])

    def ts(self, eng, out, in0, s1, op0, s2=None, op1=None, accum_out=None):
        rd = [in0]
        if not isinstance(s1, (int, float)):
            rd.append(s1)
        if s2 is not None and not isinstance(s2, (int, float)):
            rd.append(s2)
        wr = [out]
        kw = {}
        if op1 is not None:
            kw["op1"] = op1
        if accum_out is not None:
            kw["accum_out"] = accum_out
            wr.append(accum_out)
        return self.s.add(eng, lambda e: e.tensor_scalar(out, in0, s1, s2, op0, **kw), rd, wr)

    def stt(self, out, in0, scalar, in1, op0, op1):
        rd = [in0, in1]
        if not isinstance(scalar, (int, float)):
            rd.append(scalar)
        return self.s.add("dve", lambda e: e.scalar_tensor_tensor(out, in0, scalar, in1, op0, op1), rd, [out])

    def copy(self, eng, out, in_):
        if eng == "act":
            return self.s.add("act", lambda e: e.copy(out, in_), [in_], [out])
        return self.s.add(eng, lambda e: e.tensor_copy(out, in_), [in_], [out])

    def memset(self, eng, out, val):
        return self.s.add(eng, lambda e: e.memset(out, val), [], [out])

    def reduce(self, out, in_, op=None, axis=None):
        return self.s.add("dve", lambda e: e.tensor_reduce(out, in_, axis if axis is not None else AX.X, op if op is not None else ALU.add), [in_], [out])

    def recip(self, out, in_):
        return self.s.add("dve", lambda e: e.reciprocal(out, in_), [in_], [out])

from contextlib import ExitStack
from concourse.bass_utils import run_bass_kernel_spmd
import ml_dtypes

T = 2048
D = 1024
NT = 16
EPS = 1e-6
ARENA_BYTES = 204800


class Arena:
    def __init__(self, ap, nbytes):
        self.ap = ap
        self.free = [(0, nbytes)]
        self.used = {}

    def alloc(self, name, nbytes):
        nbytes = (nbytes + 63) // 64 * 64
        for i, (o, n) in enumerate(self.free):
            if n >= nbytes:
                self.free[i] = (o + nbytes, n - nbytes)
                if self.free[i][1] == 0:
                    self.free.pop(i)
                self.used[name] = (o, nbytes)
                self.hw = max(getattr(self, "hw", 0), o + nbytes)
                return o
        raise RuntimeError(f"arena full allocating {name} {nbytes}: free={self.free}")

    def release(self, name):
        o, n = self.used.pop(name)
        self.free.append((o, n))
        self.free.sort()
        m = []
        for o, n in self.free:
            if m and m[-1][0] + m[-1][1] == o:
                m[-1] = (m[-1][0], m[-1][1] + n)
            else:
                m.append((o, n))
        self.free = m

    def view(self, name, shape, dtype):
        es = mybir.dt.size(dtype)
        n = _prod(shape) * es
        o = self.alloc(name, n)
        v = self.ap[:, o // 2:(o + n) // 2]
        if dtype != BF16:
            v = v.bitcast(dtype)
        if len(shape) == 2:
            v = v.rearrange("p (a b) -> p a b", b=shape[1])
        elif len(shape) == 3:
            v = v.rearrange("p (a b c) -> p a b c", b=shape[1], c=shape[2])
        return v


def bcl(ap, n):
    sh = list(ap.shape)
    return ap.unsqueeze(len(sh)).broadcast_to(sh + [n])


IN_SHAPES = {
    "x": [T, D], "p": [T, 256], "pre_mix_norm": [D], "w_in": [D, 11296], "conv_w": [4, 4096],
    "lambda_q1": [64], "lambda_k1": [64], "lambda_q2": [64], "lambda_k2": [64],
    "da_sub_norm": [128], "gdn_a_log": [16], "gdn_dt_bias": [16], "gdn_out_norm": [128],
    "w_branch_a": [D, D], "w_branch_b": [2048, D], "w_out": [D, D], "post_mix_norm": [D],
    "pre_mlp_norm": [D], "w_up": [D, 4096], "w_down": [4096, D], "post_mlp_norm": [D],
    "w_ple": [256, D], "w_ple_gate": [D, D], "ple_norm": [D],
}


def host_consts():
    c = {}
    bf = ml_dtypes.bfloat16
    c["identb"] = np.eye(128, dtype=np.float32).astype(bf)
    c["identf"] = np.eye(128, dtype=np.float32)
    c["onesb"] = np.ones((128, 128), np.float32).astype(bf)
    c["onesf"] = np.ones((128, 128), np.float32)
    p = np.arange(128)
    same = (p[:, None] // 64) == (p[None, :] // 64)
    c["blocktri"] = (same & (p[:, None] <= p[None, :])).astype(np.float32)
    c["chunkind"] = np.stack([np.repeat((p // 64 == b)[:, None], 128, 1) for b in range(2)]).astype(np.float32)
    c["MT"] = np.where(same & (p[None, :] >= p[:, None]), 0.0, -30000.0).astype(np.float32)
    c["US"] = (same & (p[None, :] > p[:, None])).astype(np.float32)
    c["cmask"] = (p[None, :] >= p[:, None]).astype(np.float32).astype(bf)
    c["nmask"] = np.where(p[None, :] >= p[:, None], 0.0, -30000.0).astype(np.float32).astype(bf)
    return c


def host_layout(m):
    o = {}
    for n in ("pre_mix_norm", "post_mix_norm", "pre_mlp_norm", "post_mlp_norm", "ple_norm"):
        o[n + "_T"] = np.ascontiguousarray(m[n].reshape(8, 128).T)
    o["conv_w_T"] = np.ascontiguousarray(m["conv_w"].T.reshape(32, 128, 4).transpose(1, 0, 2))
    return o


LAYOUT_SHAPES = {"pre_mix_norm_T": [128, 8], "post_mix_norm_T": [128, 8], "pre_mlp_norm_T": [128, 8],
                 "post_mlp_norm_T": [128, 8], "ple_norm_T": [128, 8], "conv_w_T": [128, 32, 4]}


def build(dbg=None, upto=99):
    nc = bass.Bass("TRN2", target_bir_lowering=False)
    I = {}
    for n, sh in IN_SHAPES.items():
        I[n] = nc.dram_tensor(n, sh, F32, kind="ExternalInput").ap()
    for n, sh in LAYOUT_SHAPES.items():
        I[n] = nc.dram_tensor(n, sh, F32, kind="ExternalInput").ap()
    C = {}
    for n, a in host_consts().items():
        C[n] = nc.dram_tensor(n, list(a.shape), BF16 if a.dtype == ml_dtypes.bfloat16 else F32, kind="ExternalInput").ap()
    out = nc.dram_tensor("out", [T, D], F32, kind="ExternalOutput").ap()
    DBG = {}
    if dbg:
        for n, (sh, dt) in dbg.items():
            DBG[n] = nc.dram_tensor("dbg_" + n, sh, dt, kind="ExternalOutput").ap()

    with ExitStack() as st:
        arena_t = st.enter_context(nc.sbuf_tensor("arena", [128, ARENA_BYTES // 2], BF16))
        ps = [st.enter_context(nc.psum_tensor(f"ps{i}", [128, 512], F32)) for i in range(8)]
        sems = {e: st.enter_context(nc.semaphore("s_" + e)) for e in ENGS}
        dsems = {(q, r): st.enter_context(nc.semaphore(f"d_{q}{r}")) for q in ("sp", "act", "pool") for r in range(NDMA)}
        block = st.enter_context(nc.Block())
        k = K(nc)
        for t in ps:
            k.s.whole.add(t.name)
        A = Arena(arena_t[:], ARENA_BYTES)
        program(nc, k, A, ps, I, C, out, DBG, upto)
        k.s.emit(block, sems, dsems)
    return nc


def program(nc, k, A, ps, I, C, out, DBG, upto):
    psf = [t[:] for t in ps]
    psb = [t[:].bitcast(BF16) for t in ps]
    identb = A.view("identb", [128], BF16)
    k.dma("sp", identb, C["identb"])
    identf = A.view("identf", [128], F32)
    k.dma("sp", identf, C["identf"])
    g_premix = A.view("g_premix", [8], F32)
    k.dma("sp", g_premix, I["pre_mix_norm_T"])

    uT = A.view("uT", [8, T], BF16)
    rmsnorm_T(k, A, psb, identb, lambda tau: I["x"][tau * 128:(tau + 1) * 128, :], g_premix, uT, "p1")
    if "uT" in DBG:
        k.dma("sp", DBG["uT"].rearrange("(k p) t -> p k t", p=128), uT)
    if upto <= 1:
        return
    Cs = {"identb": identb, "identf": identf}
    for n, sh, dt in (("onesb", [128], BF16), ("onesf", [128], F32), ("blocktri", [128], F32), ("MT", [128], F32),
                      ("US", [128], F32), ("nmask", [128], BF16)):
        Cs[n] = A.view(n, sh, dt)
        k.dma("sp", Cs[n], C[n])
    Cs["chunkind"] = A.view("chunkind", [2, 128], F32)
    k.dma("sp", Cs["chunkind"], C["chunkind"].rearrange("b p i -> p b i"))
    scr = nc.dram_tensor("scr_fm", [NT, 128, 48, 128], BF16).ap()
    obscr = nc.dram_tensor("scr_ob", [16, 128, T], BF16).ap()
    gdn_phase2(nc, k, A, psf, psb, I, Cs, uT, scr)
    if "scr" in DBG:
        k.dma("sp", DBG["scr"], scr)
    S = gdn_scalars(nc, k, A, psf, I, Cs, uT)
    for n in ("beta", "gc", "kds"):
        if n in DBG:
            k.dma("sp", DBG[n].rearrange("(t p) h -> p t h", p=128), S[n])
    if upto <= 2:
        return
    uscr = nc.dram_tensor("scr_uT", [128, 8, T], BF16).ap()
    k.dma("sp", uscr, uT)
    A.release("uT")
    gdn_main(nc, k, A, psf, psb, I, Cs, S, scr, obscr, DBG)
    uT = A.view("uT", [8, T], BF16)
    k.dma("sp", uT, uscr)
    if "obscr" in DBG:
        k.dma("sp", DBG["obscr"], obscr)
    if upto <= 3:
        return
    oaT = A.view("oaT", [8, T], BF16)
    da_phase(nc, k, A, psf, psb, I, Cs, uT, oaT)
    if "oaT" in DBG:
        k.dma("sp", DBG["oaT"].rearrange("(k p) t -> p k t", p=128), oaT)
    if upto <= 4:
        return
    for n in ("gs_beta", "gs_nbeta", "gs_gc", "gs_eg", "gs_kds", "gs_EGL"):
        A.release(n)
    tail_phases(nc, k, A, psf, psb, I, Cs, uT, oaT, obscr, out, DBG, upto)


def rmsnorm_T(k, A, psb, identb, src_fn, gT, dstT, tag, src_sbuf=False):
    ss = A.view(tag + "ss", [NT], F32)
    rs = A.view(tag + "rs", [NT], F32)
    junk = A.view(tag + "junk", [D], BF16)
    xb = [A.view(f"{tag}xb{i}", [D], F32) for i in range(2)] if not src_sbuf else None
    xn = [A.view(f"{tag}xn{i}", [D], BF16) for i in range(2)]
    for tau in range(NT):
        if src_sbuf:
            xt = src_fn(tau)
        else:
            xt = xb[tau % 2]
            k.dma("sp", xt, src_fn(tau))
        k.act(junk, xt, AF.Square, accum_out=ss[:, tau:tau + 1])
        k.act(rs[:, tau:tau + 1], ss[:, tau:tau + 1], AF.Ln, scale=1.0 / D, bias=EPS)
        k.act(rs[:, tau:tau + 1], rs[:, tau:tau + 1], AF.Exp, scale=-0.5)
        xnt = xn[tau % 2]
        k.ts("dve", xnt, xt, rs[:, tau:tau + 1], ALU.mult)
        for half in range(2):
            bank = psb[half + 0]
            for j in range(4):
                kk = half * 4 + j
                k.tr(bank[:, j * 128:(j + 1) * 128], xnt[:, kk * 128:(kk + 1) * 128], identb)
            src = bank[:, 0:512].rearrange("p (a b) -> p a b", b=128)
            k.tt("dve" if half == 0 else "dve", dstT[:, half * 4:half * 4 + 4, tau * 128:(tau + 1) * 128], src,
                 bcl(gT[:, half * 4:half * 4 + 4], 128), ALU.mult)
    for n in ("ss", "rs", "junk"):
        A.release(tag + n)
    for i in range(2):
        if not src_sbuf:
            A.release(f"{tag}xb{i}")
        A.release(f"{tag}xn{i}")


class Banks:
    def __init__(self, psf, psb, ids):
        self.psf, self.psb, self.ids, self.i = psf, psb, list(ids), 0

    def next(self):
        b = self.ids[self.i % len(self.ids)]
        self.i += 1
        return b


def wview(W, c0, n):
    return W[:, c0:c0 + n].rearrange("(kk p) c -> p kk c", p=128)


def gdn_phase2(nc, k, A, psf, psb, I, Cs, uT, scr):
    PB = Banks(psf, psb, [0, 1, 2, 3])
    OB = Banks(psf, psb, [4, 5, 6, 7])
    cw = A.view("cw", [32, 4], F32)
    k.dma("sp", cw, I["conv_w_T"])
    nh = A.view("p2nh", [512], F32)
    k.memset("pool", nh, -0.5)
    wb4 = [A.view(f"p2w{i}", [8, 512], BF16) for i in range(4)]
    seq = []
    for i in range(0, 32, 2):
        seq += [i, i + 1, 32 + i // 2]
    praw = [A.view(f"praw{i}", [3 + T + 1], F32) for i in range(2)]
    for i in range(2):
        k.memset("pool", praw[i][:, 0:3], 0.0)
    ycv = A.view("ycv", [T], F32)
    svb = [A.view(f"sv{i}", [T], F32) for i in range(4)]
    sqb = [A.view(f"sq{i}", [T], BF16) for i in range(4)]
    rsdb = [A.view(f"rsd{i}", [T], F32) for i in range(2)]
    res = [A.view(f"res{i}", [T], BF16) for i in range(6)]

    def loadw(g):
        buf = wb4[g % 2] if g < 8 else wb4[2 + g % 2]
        k.dma("pool", buf, wview(I["w_in"], 3072 + g * 512, 512))

    def store(c, r):
        k.dma("sp", scr[:, :, c, :].rearrange("t p j -> p t j"), r.rearrange("p (t j) -> p t j", j=128))

    def X(c):
        g = c // 4
        if c % 4 == 0 and g + 1 < 12 and g + 1 != 8:
            loadw(g + 1)
        wt = wb4[g % 2] if g < 8 else wb4[2 + g % 2]
        r = res[pos[c] % 6]
        pr = praw[c % 2]
        for t in range(4):
            b = PB.next()
            for kk in range(8):
                k.mm(psf[b], wt[:, kk, (c % 4) * 128:(c % 4 + 1) * 128], uT[:, kk, t * 512:(t + 1) * 512], start=(kk == 0), stop=(kk == 7))
            if c >= 32:
                k.act(r[:, t * 512:(t + 1) * 512], psf[b], AF.Silu)
            else:
                k.copy("act", pr[:, 3 + t * 512:3 + (t + 1) * 512], psf[b])
        if c >= 32:
            store(c, r)

    def Y1(c):
        if c >= 32:
            return
        pr = praw[c % 2]
        r = res[pos[c] % 6]
        k.ts("dve", ycv, pr[:, 0:T], cw[:, c, 0:1], ALU.mult)
        for j in range(1, 4):
            k.stt(ycv, pr[:, j:j + T], cw[:, c, j:j + 1], ycv, ALU.mult, ALU.add)
        if c >= 16:
            k.act(r, ycv, AF.Silu)
            store(c, r)
        else:
            k.act(svb[c % 4], ycv, AF.Silu)
            k.tt("pool", sqb[c % 4], svb[c % 4], svb[c % 4], ALU.mult)

    def Y2(c):
        if c < 0 or c >= 16:
            return
        sv, sq, r = svb[c % 4], sqb[c % 4], res[pos[c] % 6]
        rsd = rsdb[c % 2]
        for t in range(4):
            sl = slice(t * 512, (t + 1) * 512)
            b = OB.next()
            k.mm(psf[b], Cs["onesb"], sq[:, sl])
            qs = 128.0 if c < 8 else 1.0
            k.act(rsd[:, sl], psf[b], AF.Ln, bias=EPS * qs, scale=qs)
            k.act(rsd[:, sl], rsd[:, sl], AF.Exp, scale=-0.5)
        k.tt("pool", r, sv, rsd, ALU.mult)
        store(c, r)

    pos = {c: p for p, c in enumerate(seq)}
    loadw(0)
    loadw(8)
    X(seq[0])
    for p in range(48):
        if p + 1 < 48:
            X(seq[p + 1])
        if p % 2 == 0:
            for pp in (p - 3, p - 2):
                if pp >= 0:
                    Y2(seq[pp])
        Y1(seq[p])
    for n in (["cw", "p2nh", "ycv", "rsd0", "rsd1"] + [f"p2w{i}" for i in range(4)] + [f"praw{i}" for i in range(2)] + [f"res{i}" for i in range(6)]
              + [f"sv{i}" for i in range(4)] + [f"sq{i}" for i in range(4)]):
        A.release(n)


def gdn_scalars(nc, k, A, psf, I, Cs, uT):
    wba = A.view("wba", [8, 32], BF16)
    k.dma("pool", wba, wview(I["w_in"], 9216, 32))
    bank = psf[4]
    for tau in range(NT):
        for kk in range(8):
            k.mm(bank[:, tau * 32:(tau + 1) * 32], uT[:, kk, tau * 128:(tau + 1) * 128], wba[:, kk, :], start=(kk == 0), stop=(kk == 7))
    ba = A.view("ba", [16, 32], F32)
    k.copy("dve", ba, bank.rearrange("p (t c) -> p t c", c=32))
    S = {}
    for n in ("beta", "nbeta", "gc", "eg", "kds", "g", "spx", "glown"):
        S[n] = A.view("gs_" + n, [16, 16], F32)
    S["EGL"] = A.view("gs_EGL", [2, 16, 16], F32)
    GL = A.view("gs_GL", [2, 16, 16], F32)
    dtb = A.view("dtb", [16], F32)
    k.dma("sp", dtb, I["gdn_dt_bias"].unsqueeze(0).partition_broadcast(128)[:, 0, :])
    negA = A.view("negA", [16], F32)
    k.dma("sp", negA, I["gdn_a_log"].unsqueeze(0).partition_broadcast(128)[:, 0, :])
    k.act(negA, negA, AF.Exp)
    k.ts("dve", negA, negA, -1.0, ALU.mult)
    k.act(S["beta"], ba[:, :, 0:16], AF.Exp, scale=-1.0)
    k.ts("dve", S["beta"], S["beta"], 1.0, ALU.add)
    k.recip(S["beta"], S["beta"])
    k.ts("dve", S["nbeta"], S["beta"], -1.0, ALU.mult)
    k.tt("dve", S["spx"], ba[:, :, 16:32], dtb.unsqueeze(1).broadcast_to([128, 16, 16]), ALU.add)
    k.act(S["spx"], S["spx"], AF.Exp)
    k.act(S["spx"], S["spx"], AF.Ln, bias=1.0)
    k.tt("dve", S["g"], S["spx"], negA.unsqueeze(1).broadcast_to([128, 16, 16]), ALU.mult)
    g2 = S["g"].rearrange("p t h -> p (t h)")
    k.mm(psf[5][:, 0:256], Cs["blocktri"], g2)
    k.copy("dve", S["gc"], psf[5][:, 0:256].rearrange("p (t h) -> p t h", h=16))
    for b in range(2):
        k.mm(psf[6][:, b * 256:(b + 1) * 256], Cs["chunkind"][:, b, :], g2)
    k.copy("dve", GL, psf[6].rearrange("p (b t h) -> p b t h", b=2, h=16))
    k.act(S["EGL"], GL, AF.Exp)
    k.act(S["eg"], S["gc"], AF.Exp)
    k.copy("dve", S["glown"][0:64], GL[0:64, 0])
    k.copy("dve", S["glown"][64:128], GL[64:128, 1])
    k.tt("dve", S["kds"], S["glown"], S["gc"], ALU.subtract)
    k.act(S["kds"], S["kds"], AF.Exp)
    for n in ("wba", "ba", "dtb", "negA", "gs_GL", "gs_g", "gs_spx", "gs_glown"):
        A.release(n)
    return S


def gdn_main(nc, k, A, psf, psb, I, Cs, S, scr, obscr, dbg=None):
    B = Banks(psf, psb, range(8))
    identb, identf, onesb, onesf, MT, US = (Cs[n] for n in ("identb", "identf", "onesb", "onesf", "MT", "US"))
    gon = A.view("gon", [1], F32)
    k.dma("sp", gon, I["gdn_out_norm"].rearrange("(p o) -> p o", o=1))
    Sf = A.view("Sf", [16, 128], F32)
    Sb = A.view("Sb", [16, 128], BF16)
    k.memset("pool", Sf, 0.0)
    k.memset("pool", Sb, 0.0)
    qkvb = [A.view(f"fm{i}", [32, 128], BF16) for i in range(2)]
    szb = [A.view(f"sz{i}", [16, 128], BF16) for i in range(2)]
    DTb = [A.view(f"DT{i}", [16, 128], F32) for i in range(2)]
    DsBb = [A.view(f"DsB{i}", [16, 128], F32) for i in range(2)]
    DBUF = ("kdec", "Wb", "nwT", "intraT", "qgT")
    dbl = [{n: A.view(f"d{i}_{n}", [16, 128], BF16) for n in DBUF} for i in range(2)]
    upb = [A.view(f"d{i}_up", [16, 128], F32) for i in range(2)]
    vtmp = A.view("s_vtmp", [4, 128], F32)
    vnew = A.view("s_vnew", [16, 128], BF16)
    osb = A.view("s_osb", [16, 128], F32)
    sqo = A.view("s_sqo", [16, 128], F32)
    _o = A.used["s_sqo"][0]
    on = A.ap[:, _o // 2:_o // 2 + 2048].rearrange("p (a b) -> p a b", b=128)
    obt = A.view("s_obt", [16, 128], BF16)
    sso = A.view("s_sso", [16], F32)

    def g4(ap3):
        return ap3.rearrange("p (g r) i -> p g r i", r=2)

    def rep2(ap3):
        sh = list(ap3.shape)
        return ap3.unsqueeze(2).broadcast_to([sh[0], sh[1], 2, sh[2]])

    def f2(ap3):
        return ap3.rearrange("p h i -> p (h i)")

    def parA(tau):
        sc = {n: S[n][:, tau, :] for n in ("nbeta", "gc")}
        Rg = A.view("t_Rg", [16, 128], F32)
        MG = A.view("t_MG", [16, 128], F32)
        k.tt("pool", Rg, identf.unsqueeze(1).broadcast_to([128, 16, 128]), bcl(sc["gc"], 128), ALU.mult)
        k.tt("pool", MG, MT.unsqueeze(1).broadcast_to([128, 16, 128]), bcl(sc["gc"], 128), ALU.subtract)
        yield
        DT = DTb[tau % 2]
        for hg in range(4):
            b = B.next()
            k.mm(psf[b], onesf, f2(Rg[:, 4 * hg:4 * hg + 4, :]), start=True, stop=False)
            k.mm(psf[b], identf, f2(MG[:, 4 * hg:4 * hg + 4, :]), start=False, stop=True)
            k.act(f2(DT[:, 4 * hg:4 * hg + 4, :]), psf[b], AF.Exp)
        A.release("t_Rg")
        A.release("t_MG")
        yield
        DsB = DsBb[tau % 2]
        k.tt("pool", DsB, DT, US.unsqueeze(1).broadcast_to([128, 16, 128]), ALU.mult)
        k.tt("pool", DsB, DsB, bcl(sc["nbeta"], 128), ALU.mult)
        yield

    def par(tau):
        fm = qkvb[tau % 2]
        d = dbl[tau % 2]
        if tau + 1 < NT:
            k.dma("sp", qkvb[(tau + 1) % 2], scr[tau + 1][:, 0:32, :])
        k.dma("sp", szb[tau % 2], scr[tau][:, 32:48, :])
        qT = fm[:, 0:8, :]
        kT = fm[:, 8:16, :]
        vT = fm[:, 16:32, :]
        sc = {n: S[n][:, tau, :] for n in ("beta", "nbeta", "gc", "eg", "kds")}
        V = lambda n, sh, dt: A.view(f"t_{n}", sh, dt)
        ktm = V("ktm", [8, 128], BF16)
        b = B.next()
        for g in range(8):
            k.tr(psb[b][:, g * 128:(g + 1) * 128], kT[:, g, :], identb)
        k.copy("act", ktm, psb[b].rearrange("p (g i) -> p g i", i=128))
        vtm = V("vtm", [16, 128], BF16)
        for hh in range(2):
            b = B.next()
            for j in range(8):
                k.tr(psb[b][:, j * 128:(j + 1) * 128], vT[:, hh * 8 + j, :], identb)
            k.copy("act" if hh == 0 else "dve", vtm[:, hh * 8:hh * 8 + 8, :], psb[b].rearrange("p (g i) -> p g i", i=128))
        kg = V("kg", [16, 128], BF16)
        kdec = d["kdec"]
        k.tt("pool", g4(kg), rep2(ktm), g4(bcl(sc["eg"], 128)), ALU.mult)
        k.tt("pool", g4(kdec), rep2(ktm), g4(bcl(sc["kds"], 128)), ALU.mult)
        yield
        DT = DTb[tau % 2]
        DsB = DsBb[tau % 2]
        Re = V("Re", [16, 128], BF16)
        k.tt("dve", Re, identb.unsqueeze(1).broadcast_to([128, 16, 128]), bcl(sc["eg"], 128), ALU.mult)
        Abuf = [V("A0", [16, 128], BF16), V("A1", [16, 128], BF16)]
        ATbuf = [V("AT0", [16, 128], BF16), V("AT1", [16, 128], BF16)]
        intraT = d["intraT"]
        for gb in range(2):
            b = B.next()
            for j in range(4):
                g = gb * 4 + j
                k.mm(psf[b][:, j * 128:(j + 1) * 128], kT[:, g, :], kT[:, g, :])
            k.tt("dve", g4(Abuf[0][:, 8 * gb:8 * gb + 8, :]), rep2(psf[b].rearrange("p (g i) -> p g i", i=128)),
                 g4(DsB[:, 8 * gb:8 * gb + 8, :]), ALU.mult)
            b = B.next()
            for j in range(4):
                g = gb * 4 + j
                k.mm(psf[b][:, j * 128:(j + 1) * 128], kT[:, g, :], qT[:, g, :])
            k.tt("dve", g4(intraT[:, 8 * gb:8 * gb + 8, :]), rep2(psf[b].rearrange("p (g i) -> p g i", i=128)),
                 g4(DT[:, 8 * gb:8 * gb + 8, :]), ALU.mult)
        yield
        Wb = d["Wb"]
        k.tt("pool", Wb, Abuf[0], identb.unsqueeze(1).broadcast_to([128, 16, 128]), ALU.add)
        for hh in range(2):
            b = B.next()
            for j in range(8):
                k.tr(psb[b][:, j * 128:(j + 1) * 128], Abuf[0][:, hh * 8 + j, :], identb)
            k.copy("act" if hh == 0 else "dve", ATbuf[0][:, hh * 8:hh * 8 + 8, :], psb[b].rearrange("p (g i) -> p g i", i=128))
        yield
        for l in range(1, 6):
            src, dst = (l - 1) % 2, l % 2
            for hg in range(4):
                hs = slice(4 * hg, 4 * hg + 4)
                bT = B.next()
                for j in range(4):
                    h = 4 * hg + j
                    k.mm(psf[bT][:, j * 128:(j + 1) * 128], Abuf[src][:, h, :], ATbuf[src][:, h, :])
                k.copy("act", f2(ATbuf[dst][:, hs, :]), psf[bT])
                if l < 5:
                    bA = B.next()
                    for j in range(4):
                        h = 4 * hg + j
                        k.mm(psf[bA][:, j * 128:(j + 1) * 128], ATbuf[src][:, h, :], Abuf[src][:, h, :])
                    k.copy("act" if hg % 2 == 0 else "dve", f2(Abuf[dst][:, hs, :]), psf[bA])
            yield
            for hg in range(4):
                hs = slice(4 * hg, 4 * hg + 4)
                bW = B.next()
                k.mm(psf[bW], identb, f2(Wb[:, hs, :]), start=True, stop=False)
                for j in range(4):
                    h = 4 * hg + j
                    k.mm(psf[bW][:, j * 128:(j + 1) * 128], ATbuf[dst][:, h, :], Wb[:, h, :], start=False, stop=(j == 3))
                k.copy("act" if hg % 2 == 0 else "dve", f2(Wb[:, hs, :]), psf[bW])
            yield
            if l == 1:
                qgT = d["qgT"]
                for hg in range(4):
                    b = B.next()
                    k.mm(psf[b], onesb, f2(Re[:, 4 * hg:4 * hg + 4, :]))
                    k.tt("dve", g4(qgT[:, 4 * hg:4 * hg + 4, :]), psf[b].rearrange("p (g r i) -> p g r i", r=2, i=128),
                         rep2(qT[:, 2 * hg:2 * hg + 2, :]), ALU.mult)
                A.release("t_Re")
                yield
        for n in ("A0", "A1", "AT0", "AT1"):
            A.release("t_" + n)
        if tau + 1 < NT:
            for _ in parA(tau + 1):
                yield
        nwT = d["nwT"]
        for hg in range(4):
            b = B.next()
            for j in range(4):
                h = 4 * hg + j
                k.mm(psf[b][:, j * 128:(j + 1) * 128], kg[:, h, :], Wb[:, h, :])
            k.act(f2(nwT[:, 4 * hg:4 * hg + 4, :]), psf[b], AF.Copy, scale=-1.0)
        yield
        up = upb[tau % 2]
        for hg in range(4):
            b = B.next()
            for j in range(4):
                h = 4 * hg + j
                k.mm(psf[b][:, j * 128:(j + 1) * 128], Wb[:, h, :], vtm[:, h, :])
            k.copy("act", f2(up[:, 4 * hg:4 * hg + 4, :]), psf[b])
        A.release("t_ktm")
        A.release("t_kg")
        A.release("t_vtm")
        yield

    def seq(tau):
        d = dbl[tau % 2]
        up = upb[tau % 2]
        szT = szb[tau % 2]
        beta = S["beta"][:, tau, :]
        kdec, Wb, nwT, intraT, qgT = (d[n] for n in DBUF)
        for cb in range(2):
            r = slice(64 * cb, 64 * cb + 64)
            for hg in range(4):
                hs = slice(4 * hg, 4 * hg + 4)
                k.tt("pool", Sf[:, hs, :], Sf[:, hs, :], bcl(S["EGL"][:, cb, tau, hs], 128), ALU.mult)
                b1 = B.next()
                for j in range(4):
                    h = 4 * hg + j
                    k.mm(psf[b1][:, j * 128:(j + 1) * 128], nwT[:, h, :], Sb[:, h, :])
                bx = B.next()
                for j in range(4):
                    h = 4 * hg + j
                    k.mm(psf[bx][:, j * 128:(j + 1) * 128], qgT[:, h, :], Sb[:, h, :])
                k.tt("dve", vtmp[r], psf[b1][r, :].rearrange("p (h i) -> p h i", i=128), up[r, hs, :], ALU.add)
                k.tt("dve", vnew[r, hs, :], vtmp[r], bcl(beta[r, hs], 128), ALU.mult)
                k.copy("act", f2(osb[r, hs, :]), psf[bx][r, :])
            yield
            for hg in range(4):
                hs = slice(4 * hg, 4 * hg + 4)
                b3 = B.next()
                for j in range(4):
                    h = 4 * hg + j
                    k.mm(psf[b3][:, j * 128:(j + 1) * 128], kdec[r, h, :], vnew[r, h, :])
                by = B.next()
                for j in range(4):
                    h = 4 * hg + j
                    k.mm(psf[by][:, j * 128:(j + 1) * 128], intraT[r, h, :], vnew[r, h, :])
                sfs = Sf[:, hs, :]
                k.tt("dve", sfs, sfs, psf[b3].rearrange("p (h i) -> p h i", i=128), ALU.add)
                k.copy("act", Sb[:, hs, :], sfs)
                k.tt("dve", osb[r, hs, :], osb[r, hs, :], psf[by][r, :].rearrange("p (h i) -> p h i", i=128), ALU.add)
            yield
        k.tt("pool", sqo, osb, osb, ALU.mult)
        k.reduce(sso, sqo)
        k.act(sso, sso, AF.Ln, scale=1.0 / 128, bias=EPS)
        k.act(sso, sso, AF.Exp, scale=-0.5)
        k.tt("dve", on, osb, bcl(sso, 128), ALU.mult)
        if dbg is not None and "osb" in dbg:
            k.dma("sp", dbg["osb"][tau * 128:(tau + 1) * 128, :].rearrange("p (h i) -> p h i", i=128), osb)
        yield
        for hh in range(2):
            b = B.next()
            for j in range(8):
                k.tr(psb[b][:, j * 128:(j + 1) * 128], on[:, hh * 8 + j, :], identb)
            k.stt(f2(obt[:, hh * 8:hh * 8 + 8, :]), psb[b], gon[:, 0:1],
                  f2(szT[:, hh * 8:hh * 8 + 8, :]), ALU.mult, ALU.mult)
        k.dma("sp", obscr[:, :, tau * 128:(tau + 1) * 128].rearrange("h p i -> p h i"), obt)
        yield

    def drive(gp, gs, ratio):
        dp = ds = False
        while not (dp and ds):
            for _ in range(ratio):
                if not dp:
                    try:
                        next(gp)
                    except StopIteration:
                        dp = True
            if not ds:
                try:
                    next(gs)
                except StopIteration:
                    ds = True

    k.dma("sp", qkvb[0], scr[0][:, 0:32, :])
    for _ in parA(0):
        pass
    for _ in par(0):
        pass
    for tau in range(NT):
        gp = par(tau + 1) if tau + 1 < NT else iter(())
        drive(gp, seq(tau), 3)
    for n in ["gon", "Sf", "Sb", "fm0", "fm1", "sz0", "sz1", "DT0", "DT1", "DsB0", "DsB1", "s_vnew", "s_osb", "s_sqo", "s_obt", "s_sso", "s_vtmp", "d0_up", "d1_up"] + [f"d{i}_{n}" for i in range(2) for n in DBUF]:
        A.release(n)


def da_phase(nc, k, A, psf, psb, I, Cs, uT, oaT):
    identb, onesb, nmask = Cs["identb"], Cs["onesb"], Cs["nmask"]
    ACC = [(0, 1), (2, 3)]
    SB_ = Banks(psf, psb, [4, 5, 6, 7])
    MB = Banks(psf, psb, [7, 6, 5, 4])
    lv = A.view("lv", [4, 64], F32)
    for i, n in enumerate(("lambda_q1", "lambda_k1", "lambda_q2", "lambda_k2")):
        k.dma("sp", lv[:, i, :], I[n].unsqueeze(0).partition_broadcast(128)[:, 0, :])
    lp = A.view("lp", [2, 64], F32)
    k.tt("dve", lp[:, 0, :], lv[:, 0, :], lv[:, 1, :], ALU.mult)
    k.tt("dve", lp[:, 1, :], lv[:, 2, :], lv[:, 3, :], ALU.mult)
    ls = A.view("ls", [2], F32)
    k.reduce(ls, lp)
    k.act(ls, ls, AF.Exp)
    nlam = A.view("nlam", [1], F32)
    k.tt("dve", nlam, ls[:, 1:2], ls[:, 0:1], ALU.subtract)
    k.ts("dve", nlam, nlam, -0.2, ALU.add)
    gsub = A.view("gsub", [1], F32)
    k.dma("sp", gsub, I["da_sub_norm"].rearrange("(p o) -> p o", o=1))
    k.ts("dve", gsub, gsub, 0.8, ALU.mult)
    Vall = A.view("da_V", [NT, 1024], BF16)
    wv = [A.view(f"da_wv{i}", [8, 512], BF16) for i in range(2)]
    for half in range(2):
        k.dma("pool", wv[half], wview(I["w_in"], 2048 + half * 512, 512))
    for half in range(2):
        for tau in range(NT):
            b = SB_.next()
            for kk in range(8):
                k.mm(psf[b], uT[:, kk, tau * 128:(tau + 1) * 128], wv[half][:, kk, :], start=(kk == 0), stop=(kk == 7))
            k.copy("act" if tau % 2 == 0 else "dve", Vall[:, tau, half * 512:(half + 1) * 512], psf[b])
    wq = [A.view(f"daw{i}", [8, 128], BF16) for i in range(4)]
    qTb = [A.view(f"da_qT{i}", [T], BF16) for i in range(2)]
    kTb = [A.view(f"da_kT{i}", [T], BF16) for i in range(2)]
    oacc = A.view("da_oacc", [T], F32)
    pts = [A.view(f"da_pt{i}", [512], BF16) for i in range(6)]
    rz = A.view("da_rz", [512], F32)
    rz2 = A.view("da_rz2", [512], F32)
    tmpo = A.view("da_tmpo", [512], F32)
    sq = A.view("da_sq", [T], BF16)
    rsd = A.view("da_rsd", [512], F32)
    st = {"w": 0, "pt": 0}

    def proj(h):
        for which, dst, col0 in (("q", qTb[h % 2], h * 128), ("k", kTb[h % 2], 1024 + h * 128)):
            wt = wq[st["w"] % 4]
            st["w"] += 1
            k.dma("pool", wt, wview(I["w_in"], col0, 128))
            for t in range(4):
                b = MB.next()
                for kk in range(8):
                    k.mm(psf[b], wt[:, kk, :], uT[:, kk, t * 512:(t + 1) * 512], start=(kk == 0), stop=(kk == 7))
                if which == "q":
                    k.ts("dve", dst[:, t * 512:(t + 1) * 512], psf[b], 0.125, ALU.mult)
                else:
                    k.copy("dve", dst[:, t * 512:(t + 1) * 512], psf[b])

    def attn(h, qg):
        qT, kT = qTb[h % 2], kTb[h % 2]
        nk = 4 * qg + 4
        banks = {0: (0, 1), 1: (2, 3)}

        def scores(ki):
            q0 = max(ki * 128, qg * 512)
            nq = (qg + 1) * 512 - q0
            diag = ki >= 4 * qg
            items = []
            bs = [SB_.next(), SB_.next()]
            for j in range(2):
                pr = slice(64 * j, 64 * j + 64)
                k.mm(psf[bs[j]][:, 0:nq], kT[pr, ki * 128:(ki + 1) * 128], qT[pr, q0:q0 + nq], start=True, stop=not diag)
            for j in range(2):
                if diag:
                    k.mm(psf[bs[j]][:, 0:128], identb, nmask, start=False, stop=True)
                pt = pts[st["pt"] % 6]
                st["pt"] += 1
                k.act(pt[:, 0:nq], psf[bs[j]][:, 0:nq], AF.Exp)
                items.append((j, ki, q0 - qg * 512, nq, pt))
            return items

        def av(items):
            for j, ki, c0, nq, pt in items:
                bO, bZ = banks[j]
                k.mm(psf[bO][:, c0:c0 + nq], Vall[:, ki, h * 128:(h + 1) * 128], pt[:, 0:nq], start=(ki == 0), stop=(ki == nk - 1))
                k.mm(psf[bZ][:, c0:c0 + nq], onesb, pt[:, 0:nq], start=(ki == 0), stop=(ki == nk - 1))

        q = []
        for ki in range(nk):
            q.append(scores(ki))
            if len(q) > 1:
                av(q.pop(0))
        while q:
            av(q.pop(0))
        cols = slice(qg * 512, (qg + 1) * 512)
        k.act(rz, psf[1], AF.Ln)
        k.act(rz, rz, AF.Exp, scale=-1.0)
        k.tt("dve", oacc[:, cols], psf[0], rz, ALU.mult)
        k.act(rz2, psf[3], AF.Ln)
        k.act(rz2, rz2, AF.Exp, scale=-1.0)
        k.tt("dve", tmpo, psf[2], rz2, ALU.mult)
        k.stt(oacc[:, cols], tmpo, nlam[:, 0:1], oacc[:, cols], ALU.mult, ALU.add)

    def finalize(h):
        k.act(sq, oacc, AF.Square)
        for t in range(4):
            tsl = slice(t * 512, (t + 1) * 512)
            b = MB.next()
            k.mm(psf[b], onesb, sq[:, tsl])
            k.act(rsd, psf[b], AF.Ln, scale=1.0 / 128, bias=EPS)
            k.act(rsd, rsd, AF.Exp, scale=-0.5)
            k.stt(oaT[:, h, tsl], oacc[:, tsl], gsub[:, 0:1], rsd, ALU.mult, ALU.mult)

    proj(0)
    for h in range(8):
        for qg in range(4):
            attn(h, qg)
            if qg == 1 and h + 1 < 8:
                proj(h + 1)
        finalize(h)
    for n in (["lv", "lp", "ls", "nlam", "gsub", "da_V", "da_wv0", "da_wv1", "da_oacc", "da_rz", "da_rz2", "da_tmpo", "da_sq", "da_rsd"]
              + [f"daw{i}" for i in range(4)] + [f"da_pt{i}" for i in range(6)] + [f"da_qT{i}" for i in range(2)] + [f"da_kT{i}" for i in range(2)]):
        A.release(n)


def bcast_vec(k, A, name, vec):
    t = A.view(name, [int(vec.shape[0])], F32)
    k.dma("sp", t, vec.unsqueeze(0).partition_broadcast(128)[:, 0, :])
    return t


NHALF = [None]


def norm_res(k, A, src2, gain_b, resid, out_ap, tag):
    ss2 = A.view(tag + "ss2", [2], F32)
    junk = A.view(tag + "junk", [512], BF16)
    rs = A.view(tag + "rs", [1], F32)
    for h in range(2):
        k.act(junk, src2[h], AF.Square, accum_out=ss2[:, h:h + 1])
    k.tt("dve", rs, ss2[:, 0:1], ss2[:, 1:2], ALU.add)
    k.ts("dve", rs, rs, 1.0 / D, ALU.mult, EPS, ALU.add)
    k.tt("pool", rs, rs, NHALF[0], ALU.pow)
    for h in range(2):
        sl = slice(h * 512, (h + 1) * 512)
        k.stt(out_ap[:, sl], src2[h], rs[:, 0:1], gain_b[:, sl], ALU.mult, ALU.mult)
    k.tt("dve", out_ap, out_ap, resid, ALU.add)
    for n in ("ss2", "junk", "rs"):
        A.release(tag + n)


def tail_phases(nc, k, A, psf, psb, I, Cs, uT, oaT, obscr, out, DBG, upto):
    identb = Cs["identb"]
    B = Banks(psf, psb, range(8))
    nh = A.view("nhalf", [1], F32)
    k.memset("pool", nh, -0.5)
    NHALF[0] = nh
    obT = A.view("obT", [16, T], BF16)
    for h in range(16):
        k.dma("sp", obT[:, h, :], obscr[h])
    mergedT = A.view("mergedT", [8, T], BF16)
    wts = {}
    for n, nk in (("wa", 8), ("wb", 16), ("wga", 8), ("wgb", 8)):
        wts[n] = [A.view(f"{n}{i}", [nk, 128], BF16) for i in range(2)]
    sg = [A.view(f"sg{i}", [512], F32) for i in range(2)]
    m1 = A.view("m1", [512], F32)
    m2 = A.view("m2", [512], F32)

    def load_merge_w(j):
        k.dma("pool", wts["wga"][j % 2], wview(I["w_in"], 9248 + j * 128, 128))
        k.dma("pool", wts["wgb"][j % 2], wview(I["w_in"], 10272 + j * 128, 128))
        k.dma("pool", wts["wa"][j % 2], wview(I["w_branch_a"], j * 128, 128))
        k.dma("pool", wts["wb"][j % 2], wview(I["w_branch_b"], j * 128, 128))

    load_merge_w(0)
    for j in range(8):
        if j + 1 < 8:
            load_merge_w(j + 1)
        wa, wb, wga, wgb = (wts[n][j % 2] for n in ("wa", "wb", "wga", "wgb"))
        for t in range(4):
            tsl = slice(t * 512, (t + 1) * 512)
            bA, bB, bGa, bGb = B.next(), B.next(), B.next(), B.next()
            for kk in range(8):
                k.mm(psf[bGa], wga[:, kk, :], uT[:, kk, tsl], start=(kk == 0), stop=(kk == 7))
            for kk in range(8):
                k.mm(psf[bGb], wgb[:, kk, :], uT[:, kk, tsl], start=(kk == 0), stop=(kk == 7))
            for kk in range(8):
                k.mm(psf[bA], wa[:, kk, :], oaT[:, kk, tsl], start=(kk == 0), stop=(kk == 7))
            for kk in range(16):
                k.mm(psf[bB], wb[:, kk, :], obT[:, kk, tsl], start=(kk == 0), stop=(kk == 15))
            k.act(sg[0], psf[bGa], AF.Sigmoid)
            k.act(sg[1], psf[bGb], AF.Sigmoid)
            k.tt("dve", m1, psf[bA], sg[0], ALU.mult)
            k.tt("dve", m2, psf[bB], sg[1], ALU.mult)
            k.tt("pool", mergedT[:, j, tsl], m1, m2, ALU.add)
    for n in ["obT", "m1", "m2", "sg0", "sg1"] + [f"{n}{i}" for n in ("wa", "wb", "wga", "wgb") for i in range(2)]:
        A.release(n)
    A.release("uT")
    A.release("oaT")
    if "mergedT" in DBG:
        k.dma("sp", DBG["mergedT"].rearrange("(k p) t -> p k t", p=128), mergedT)
    wout = A.view("wout", [8, D], BF16)
    for kk in range(8):
        k.dma("pool", wout[:, kk, :], I["w_out"][kk * 128:(kk + 1) * 128, :])
    wup = A.view("wup", [8, 4096], BF16)
    for kk in range(8):
        k.dma("pool", wup[:, kk, :], I["w_up"][kk * 128:(kk + 1) * 128, :])
    g_postmix = bcast_vec(k, A, "g_postmix", I["post_mix_norm"])
    g_premlp = A.view("g_premlp", [8], F32)
    k.dma("sp", g_premlp, I["pre_mlp_norm_T"])
    u2T = A.view("u2T", [8, T], BF16)
    h1b = [A.view(f"h1b{i}", [D], F32) for i in range(4)]
    xb = [A.view(f"xb{i}", [D], F32) for i in range(3)]
    ss6 = A.view("q6ss", [NT], F32)
    rs6 = A.view("q6rs", [NT], F32)
    junk6 = A.view("q6junk", [D], BF16)
    xn6 = [A.view(f"q6xn{i}", [D], BF16) for i in range(2)]

    def p6a(tau):
        ht, xt = h1b[tau % 4], xb[tau % 3]
        k.dma("sp", xt, I["x"][tau * 128:(tau + 1) * 128, :])
        bs = [B.next(), B.next()]
        for half in range(2):
            for kk in range(8):
                k.mm(psf[bs[half]], mergedT[:, kk, tau * 128:(tau + 1) * 128], wout[:, kk, half * 512:(half + 1) * 512],
                     start=(kk == 0), stop=(kk == 7))

        def fin():
            norm_res(k, A, [psf[bs[0]], psf[bs[1]]], g_postmix, xt, ht, "p6")
            k.dma("pool", out[tau * 128:(tau + 1) * 128, :], ht)
        return fin

    def p6b(tau):
        xt = h1b[tau % 4]
        k.act(junk6, xt, AF.Square, accum_out=ss6[:, tau:tau + 1])
        k.act(rs6[:, tau:tau + 1], ss6[:, tau:tau + 1], AF.Ln, scale=1.0 / D, bias=EPS)
        k.act(rs6[:, tau:tau + 1], rs6[:, tau:tau + 1], AF.Exp, scale=-0.5)
        xnt = xn6[tau % 2]
        k.ts("dve", xnt, xt, rs6[:, tau:tau + 1], ALU.mult)
        for half in range(2):
            b = B.next()
            for j in range(4):
                kk = half * 4 + j
                k.tr(psb[b][:, j * 128:(j + 1) * 128], xnt[:, kk * 128:(kk + 1) * 128], identb)
            src = psb[b][:, 0:512].rearrange("p (a b) -> p a b", b=128)
            k.tt("dve", u2T[:, half * 4:half * 4 + 4, tau * 128:(tau + 1) * 128], src,
                 bcl(g_premlp[:, half * 4:half * 4 + 4], 128), ALU.mult)

    p6a(0)()
    p6a(1)()
    for tau in range(NT):
        fin = p6a(tau + 2) if tau + 2 < NT else None
        p6b(tau)
        if fin is not None:
            fin()
    for n in ["mergedT", "wout", "g_postmix", "g_premlp", "h1b0", "h1b1", "h1b2", "h1b3", "xb0", "xb1", "xb2", "q6ss", "q6rs", "q6junk", "q6xn0", "q6xn1"]:
        A.release(n)
    if upto <= 6:
        return
    wdn = A.view("wdn", [32, D], BF16)
    for f in range(32):
        k.dma("pool", wdn[:, f, :], I["w_down"][f * 128:(f + 1) * 128, :])
    hscr = nc.dram_tensor("scr_hid", [NT, 128, 32, 128], BF16).ap()
    rl = [A.view(f"rl{i}", [512], F32) for i in range(2)]
    hd = [A.view(f"hd{i}", [T], BF16) for i in range(2)]
    for f in range(32):
        for t in range(4):
            b = B.next()
            for kk in range(8):
                k.mm(psf[b], wup[:, kk, f * 128:(f + 1) * 128], u2T[:, kk, t * 512:(t + 1) * 512], start=(kk == 0), stop=(kk == 7))
            r = rl[t % 2]
            k.act(r, psf[b], AF.Relu)
            k.tt("dve" if t % 2 == 0 else "pool", hd[f % 2][:, t * 512:(t + 1) * 512], r, r, ALU.mult)
        k.dma("sp", hscr[:, :, f, :].rearrange("t p j -> p t j"), hd[f % 2].rearrange("p (t j) -> p t j", j=128))
    for n in ["u2T", "wup"] + [f"rl{i}" for i in range(2)] + [f"hd{i}" for i in range(2)]:
        A.release(n)
    wple = A.view("wple", [2, D], BF16)
    for kk in range(2):
        k.dma("pool", wple[:, kk, :], I["w_ple"][kk * 128:(kk + 1) * 128, :])
    wpg = A.view("wpg", [8, D], BF16)
    for kk in range(8):
        k.dma("pool", wpg[:, kk, :], I["w_ple_gate"][kk * 128:(kk + 1) * 128, :])
    g_postmlp = bcast_vec(k, A, "g_postmlp", I["post_mlp_norm"])
    g_ple = bcast_vec(k, A, "g_ple", I["ple_norm"])
    hidb = [A.view(f"hidb{i}", [32, 128], BF16) for i in range(2)]
    h1t = [A.view(f"h1t{i}", [D], F32) for i in range(2)]
    h2t = [A.view(f"h2t{i}", [D], F32) for i in range(2)]
    h2bb = [A.view(f"h2b{i}", [D], BF16) for i in range(2)]
    h2T = A.view("h2T", [8, 128], BF16)
    pt32 = [A.view(f"pt32{i}", [256], F32) for i in range(2)]
    ptbb = [A.view(f"ptb{i}", [256], BF16) for i in range(2)]
    pT = A.view("pT", [2, 128], BF16)
    sgp = A.view("sgp", [D], F32)
    et = A.view("et", [D], F32)
    ot = [A.view(f"ot{i}", [D], F32) for i in range(2)]

    def p7d(tau):
        rows = slice(tau * 128, (tau + 1) * 128)
        hb = hidb[tau % 2]
        k.dma("sp", hb, hscr[tau])
        k.dma("sp", h1t[tau % 2], out[rows, :])
        k.dma("sp", pt32[tau % 2], I["p"][rows, :])
        bs = [0, 1] if tau % 2 == 0 else [2, 3]
        for half in range(2):
            for f in range(32):
                k.mm(psf[bs[half]], hb[:, f, :], wdn[:, f, half * 512:(half + 1) * 512], start=(f == 0), stop=(f == 31))
        h2 = h2t[tau % 2]

        def fin():
            norm_res(k, A, [psf[bs[0]], psf[bs[1]]], g_postmlp, h1t[tau % 2], h2, "p7")
            k.copy("act", h2bb[tau % 2], h2)
            k.copy("act", ptbb[tau % 2], pt32[tau % 2])
        return fin

    def p8e(tau):
        rows = slice(tau * 128, (tau + 1) * 128)
        h2, h2b, ptb = h2t[tau % 2], h2bb[tau % 2], ptbb[tau % 2]
        b = 4
        for kk in range(8):
            k.tr(psb[b][:, kk * 128:(kk + 1) * 128], h2b[:, kk * 128:(kk + 1) * 128], identb)
        k.copy("dve", h2T, psb[b].rearrange("p (a i) -> p a i", i=128))
        b = 5
        for kk in range(2):
            k.tr(psb[b][:, kk * 128:(kk + 1) * 128], ptb[:, kk * 128:(kk + 1) * 128], identb)
        k.copy("dve", pT, psb[b][:, 0:256].rearrange("p (a i) -> p a i", i=128))
        be = [6, 6]
        bg = [7, 7]
        for half in range(2):
            hs = slice(half * 512, (half + 1) * 512)
            for kk in range(8):
                k.mm(psf[bg[half]], h2T[:, kk, :], wpg[:, kk, hs], start=(kk == 0), stop=(kk == 7))
            for kk in range(2):
                k.mm(psf[be[half]], pT[:, kk, :], wple[:, kk, hs], start=(kk == 0), stop=(kk == 1))
            k.act(sgp[:, hs], psf[bg[half]], AF.Sigmoid)
            k.tt("dve", et[:, hs], psf[be[half]], sgp[:, hs], ALU.mult)
        o = ot[tau % 2]
        norm_res(k, A, [et[:, 0:512], et[:, 512:1024]], g_ple, h2, o, "p8")
        k.dma("pool", out[rows, :], o)

    p7d(0)()
    for tau in range(NT):
        fin = p7d(tau + 1) if tau + 1 < NT else None
        p8e(tau)
        if fin is not None:
            fin()


_NC_CACHE = {}


def kernel(**inputs):
    n = 8
    if "nc" not in _NC_CACHE:
        _NC_CACHE["nc"] = build()
    nc = _NC_CACHE["nc"]
    consts = host_consts()
    shared = {}
    for name, sh in IN_SHAPES.items():
        if name in ("x", "p"):
            continue
        shared[name] = np.ascontiguousarray(np.asarray(inputs[name], dtype=np.float32)[0].reshape(sh))
    shared.update(host_layout(shared))
    shared.update(consts)
    x = np.asarray(inputs["x"], dtype=np.float32)
    p = np.asarray(inputs["p"], dtype=np.float32)
    in_maps = []
    for b in range(n):
        m = dict(shared)
        m["x"] = np.ascontiguousarray(x[b])
        m["p"] = np.ascontiguousarray(p[0, b])
        in_maps.append(m)
    res = run_bass_kernel_spmd(nc, in_maps, core_ids=list(range(n)))
    return np.stack([np.asarray(r["out"], dtype=np.float32) for r in res.results], axis=0)
```

```python
import numpy as np
import concourse.bass as bass
import concourse.mybir as mybir

F32 = mybir.dt.float32
BF16 = mybir.dt.bfloat16
AF = mybir.ActivationFunctionType
ALU = mybir.AluOpType
AX = mybir.AxisListType

ENGS = ("pe", "act", "dve", "pool", "sp")
NDMA = 40


def _prod(xs):
    r = 1
    for v in xs:
        r *= int(v)
    return r


class Op:
    __slots__ = ("eng", "fn", "deps", "inc", "cnt", "dma", "dsem", "dval", "idx")


class Sched:
    def __init__(self, nc):
        self.nc = nc
        self.streams = {e: [] for e in ENGS}
        self.acc = {}
        self.ndma = {e: 0 for e in ENGS}
        self.whole = set()

    def rect(self, ap):
        t = ap.tensor
        name = t.name
        es = mybir.dt.size(ap.dtype)
        space = str(ap.space)
        off = int(ap.offset)
        pat = ap.ap
        if "DRAM" in space.upper() or space.upper().startswith("D"):
            lo = off
            hi = off
            for st, c in pat:
                if st >= 0:
                    hi += st * (c - 1)
                else:
                    lo += st * (c - 1)
            return name, 0, 1, lo * es, (hi + 1) * es
        pstep = _prod(t.shape[1:])
        p0 = off // pstep
        f0 = off % pstep
        np_ = pat[0][1]
        f1 = f0
        for st, c in pat[1:]:
            f1 += st * (c - 1)
        if name in self.whole:
            return name, 0, 128, 0, 1 << 30
        return name, p0, p0 + np_, f0 * es, (f1 + 1) * es

    def add(self, eng, fn, reads=(), writes=(), dma=False):
        op = Op()
        op.eng = eng
        op.fn = fn
        op.inc = False
        op.cnt = None
        op.dma = dma
        op.deps = []
        op.idx = len(self.streams[eng])
        deps = {}
        for ap in reads:
            self._access(ap, op, False, deps)
        for ap in writes:
            self._access(ap, op, True, deps)
        best = {}
        for d in deps.values():
            key = ("d",) + d.dsem if d.dma else d.eng
            cur = best.get(key)
            if cur is None or (d.dval > cur.dval if d.dma else d.idx > cur.idx):
                best[key] = d
        op.deps = list(best.values())
        for d in op.deps:
            d.inc = True
        if dma:
            i = self.ndma[eng]
            self.ndma[eng] += 1
            op.dsem = (eng, i % NDMA)
            op.dval = 16 * (i // NDMA + 1)
        self.streams[eng].append(op)
        return op

    def _need(self, p, pw, o, ow):
        if not (pw or ow):
            return False
        if p is o:
            return False
        if p.dma or o.dma:
            return True
        if p.eng == o.eng:
            if o.eng == "pe":
                return False
            return pw and not ow
        return True

    def _access(self, ap, op, is_w, deps):
        name, p0, p1, b0, b1 = self.rect(ap)
        lst = self.acc.setdefault(name, [])
        keep = []
        for ent in lst:
            q0, q1, c0, c1, pop, pw = ent
            ov = not (q1 <= p0 or p1 <= q0 or c1 <= b0 or b1 <= c0)
            if ov and self._need(pop, pw, op, is_w):
                deps[id(pop)] = pop
            covered = q0 >= p0 and q1 <= p1 and c0 >= b0 and c1 <= b1
            if pop is op:
                keep.append(ent)
            elif is_w and covered:
                continue
            elif (not is_w) and (not pw) and covered and pop.eng == op.eng and not pop.dma and not op.dma:
                continue
            else:
                keep.append(ent)
        keep.append((p0, p1, b0, b1, op, is_w))
        self.acc[name] = keep

    def emit(self, block, sems, dsems, final_engine="sp"):
        nc = self.nc
        for e in ENGS:
            c = 0
            for op in self.streams[e]:
                if op.inc and not op.dma:
                    c += 1
                    op.cnt = c
        eng_obj = {"pe": "tensor", "act": "scalar", "dve": "vector", "pool": "gpsimd", "sp": "sync"}

        def run_stream(e, engine):
            seen = {}
            for op in self.streams[e]:
                for d in op.deps:
                    if d.dma:
                        s = dsems[d.dsem]
                        key = ("d",) + d.dsem
                        val = d.dval
                    else:
                        s = sems[d.eng]
                        key = d.eng
                        val = d.cnt
                    if seen.get(key, 0) < val:
                        engine.wait_ge(s, val)
                        seen[key] = val
                if op.dma and op.dval > 16:
                    key = ("d",) + op.dsem
                    if seen.get(key, 0) < op.dval - 16:
                        engine.wait_ge(dsems[op.dsem], op.dval - 16)
                        seen[key] = op.dval - 16
                ins = op.fn(engine)
                if op.dma:
                    ins.then_inc(dsems[op.dsem], 16)
                elif op.inc:
                    ins.then_inc(sems[e], 1)
            if e == final_engine:
                for q in ("sp", "pool"):
                    n = self.ndma[q]
                    for r in range(min(n, NDMA)):
                        cntr = (n - r + NDMA - 1) // NDMA
                        engine.wait_ge(dsems[(q, r)], 16 * cntr)

        for e in ENGS:
            getattr(block, eng_obj[e])(lambda engine, e=e: run_stream(e, engine))


class K:
    def __init__(self, nc):
        self.nc = nc
        self.s = Sched(nc)

    def dma(self, q, out, in_, **kw):
        return self.s.add(q, lambda e: e.dma_start(out=out, in_=in_, **kw), [in_], [out], dma=True)

    def mm(self, out, lhsT, rhs, start=True, stop=True):
        return self.s.add("pe", lambda e: e.matmul(out, lhsT, rhs, start=start, stop=stop), [lhsT, rhs], [out])

    def tr(self, out, in_, ident):
        return self.s.add("pe", lambda e: e.transpose(out, in_, ident), [in_, ident], [out])

    def act(self, out, in_, func, bias=None, scale=None, accum_out=None, eng="act"):
        kw = {}
        rd = [in_]
        wr = [out]
        if bias is not None:
            kw["bias"] = bias
            if not isinstance(bias, (int, float)):
                rd.append(bias)
        if scale is not None:
            kw["scale"] = scale
            if not isinstance(scale, (int, float)):
                rd.append(scale)
        if accum_out is not None:
            kw["accum_out"] = accum_out
            wr.append(accum_out)
        return self.s.add("act", lambda e: e.activation(out, in_, func, **kw), rd, wr)

    def tt(self, eng, out, in0, in1, op):
        return self.s.add(eng, lambda e: e.tensor_tensor(out, in0, in1, op), [in0, in1], [out])

    def ts(self, eng, out, in0, s1, op0, s2=None, op1=None, accum_out=None):
        rd = [in0]
        if not isinstance(s1, (int, float)):
            rd.append(s1)
        if s2 is not None and not isinstance(s2, (int, float)):
            rd.append(s2)
        wr = [out]
        kw = {}
        if op1 is not None:
            kw["op1"] = op1
        if accum_out is not None:
            kw["accum_out"] = accum_out
            wr.append(accum_out)
        return self.s.add(eng, lambda e: e.tensor_scalar(out, in0, s1, s2, op0, **kw), rd, wr)

    def stt(self, out, in0, scalar, in1, op0, op1):
        rd = [in0, in1]
        if not isinstance(scalar, (int, float)):
            rd.append(scalar)
        return self.s.add("dve", lambda e: e.scalar_tensor_tensor(out, in0, scalar, in1, op0, op1), rd, [out])

    def copy(self, eng, out, in_):
        if eng == "act":
            return self.s.add("act", lambda e: e.copy(out, in_), [in_], [out])
        return self.s.add(eng, lambda e: e.tensor_copy(out, in_), [in_], [out])

    def memset(self, eng, out, val):
        return self.s.add(eng, lambda e: e.memset(out, val), [], [out])

    def reduce(self, out, in_, op=None, axis=None):
        return self.s.add("dve", lambda e: e.tensor_reduce(out, in_, axis if axis is not None else AX.X, op if op is not None else ALU.add), [in_], [out])

    def recip(self, out, in_):
        return self.s.add("dve", lambda e: e.reciprocal(out, in_), [in_], [out])

from contextlib import ExitStack
from concourse.bass_utils import run_bass_kernel_spmd
import ml_dtypes

T = 2048
D = 1024
NT = 16
EPS = 1e-6
ARENA_BYTES = 204800


class Arena:
    def __init__(self, ap, nbytes):
        self.ap = ap
        self.free = [(0, nbytes)]
        self.used = {}

    def alloc(self, name, nbytes):
        nbytes = (nbytes + 63) // 64 * 64
        for i, (o, n) in enumerate(self.free):
            if n >= nbytes:
                self.free[i] = (o + nbytes, n - nbytes)
                if self.free[i][1] == 0:
                    self.free.pop(i)
                self.used[name] = (o, nbytes)
                self.hw = max(getattr(self, "hw", 0), o + nbytes)
                return o
        raise RuntimeError(f"arena full allocating {name} {nbytes}: free={self.free}")

    def release(self, name):
        o, n = self.used.pop(name)
        self.free.append((o, n))
        self.free.sort()
        m = []
        for o, n in self.free:
            if m and m[-1][0] + m[-1][1] == o:
                m[-1] = (m[-1][0], m[-1][1] + n)
            else:
                m.append((o, n))
        self.free = m

    def view(self, name, shape, dtype):
        es = mybir.dt.size(dtype)
        n = _prod(shape) * es
        o = self.alloc(name, n)
        v = self.ap[:, o // 2:(o + n) // 2]
        if dtype != BF16:
            v = v.bitcast(dtype)
        if len(shape) == 2:
            v = v.rearrange("p (a b) -> p a b", b=shape[1])
        elif len(shape) == 3:
            v = v.rearrange("p (a b c) -> p a b c", b=shape[1], c=shape[2])
        return v


def bcl(ap, n):
    sh = list(ap.shape)
    return ap.unsqueeze(len(sh)).broadcast_to(sh + [n])


IN_SHAPES = {
    "x": [T, D], "p": [T, 256], "pre_mix_norm": [D], "w_in": [D, 11296], "conv_w": [4, 4096],
    "lambda_q1": [64], "lambda_k1": [64], "lambda_q2": [64], "lambda_k2": [64],
    "da_sub_norm": [128], "gdn_a_log": [16], "gdn_dt_bias": [16], "gdn_out_norm": [128],
    "w_branch_a": [D, D], "w_branch_b": [2048, D], "w_out": [D, D], "post_mix_norm": [D],
    "pre_mlp_norm": [D], "w_up": [D, 4096], "w_down": [4096, D], "post_mlp_norm": [D],
    "w_ple": [256, D], "w_ple_gate": [D, D], "ple_norm": [D],
}


def host_consts():
    c = {}
    bf = ml_dtypes.bfloat16
    c["identb"] = np.eye(128, dtype=np.float32).astype(bf)
    c["identf"] = np.eye(128, dtype=np.float32)
    c["onesb"] = np.ones((128, 128), np.float32).astype(bf)
    c["onesf"] = np.ones((128, 128), np.float32)
    p = np.arange(128)
    same = (p[:, None] // 64) == (p[None, :] // 64)
    c["blocktri"] = (same & (p[:, None] <= p[None, :])).astype(np.float32)
    c["chunkind"] = np.stack([np.repeat((p // 64 == b)[:, None], 128, 1) for b in range(2)]).astype(np.float32)
    c["MT"] = np.where(same & (p[None, :] >= p[:, None]), 0.0, -30000.0).astype(np.float32)
    c["US"] = (same & (p[None, :] > p[:, None])).astype(np.float32)
    c["cmask"] = (p[None, :] >= p[:, None]).astype(np.float32).astype(bf)
    c["nmask"] = np.where(p[None, :] >= p[:, None], 0.0, -30000.0).astype(np.float32).astype(bf)
    return c


def host_layout(m):
    o = {}
    for n in ("pre_mix_norm", "post_mix_norm", "pre_mlp_norm", "post_mlp_norm", "ple_norm"):
        o[n + "_T"] = np.ascontiguousarray(m[n].reshape(8, 128).T)
    o["conv_w_T"] = np.ascontiguousarray(m["conv_w"].T.reshape(32, 128, 4).transpose(1, 0, 2))
    return o


LAYOUT_SHAPES = {"pre_mix_norm_T": [128, 8], "post_mix_norm_T": [128, 8], "pre_mlp_norm_T": [128, 8],
                 "post_mlp_norm_T": [128, 8], "ple_norm_T": [128, 8], "conv_w_T": [128, 32, 4]}


def build(dbg=None, upto=99):
    nc = bass.Bass("TRN2", target_bir_lowering=False)
    I = {}
    for n, sh in IN_SHAPES.items():
        I[n] = nc.dram_tensor(n, sh, F32, kind="ExternalInput").ap()
    for n, sh in LAYOUT_SHAPES.items():
        I[n] = nc.dram_tensor(n, sh, F32, kind="ExternalInput").ap()
    C = {}
    for n, a in host_consts().items():
        C[n] = nc.dram_tensor(n, list(a.shape), BF16 if a.dtype == ml_dtypes.bfloat16 else F32, kind="ExternalInput").ap()
    out = nc.dram_tensor("out", [T, D], F32, kind="ExternalOutput").ap()
    DBG = {}
    if dbg:
        for n, (sh, dt) in dbg.items():
            DBG[n] = nc.dram_tensor("dbg_" + n, sh, dt, kind="ExternalOutput").ap()

    with ExitStack() as st:
        arena_t = st.enter_context(nc.sbuf_tensor("arena", [128, ARENA_BYTES // 2], BF16))
        ps = [st.enter_context(nc.psum_tensor(f"ps{i}", [128, 512], F32)) for i in range(8)]
        sems = {e: st.enter_context(nc.semaphore("s_" + e)) for e in ENGS}
        dsems = {(q, r): st.enter_context(nc.semaphore(f"d_{q}{r}")) for q in ("sp", "pool") for r in range(NDMA)}
        block = st.enter_context(nc.Block())
        k = K(nc)
        for t in ps:
            k.s.whole.add(t.name)
        A = Arena(arena_t[:], ARENA_BYTES)
        program(nc, k, A, ps, I, C, out, DBG, upto)
        k.s.emit(block, sems, dsems)
    return nc


def program(nc, k, A, ps, I, C, out, DBG, upto):
    psf = [t[:] for t in ps]
    psb = [t[:].bitcast(BF16) for t in ps]
    identb = A.view("identb", [128], BF16)
    k.dma("sp", identb, C["identb"])
    identf = A.view("identf", [128], F32)
    k.dma("sp", identf, C["identf"])
    g_premix = A.view("g_premix", [8], F32)
    k.dma("sp", g_premix, I["pre_mix_norm_T"])

    uT = A.view("uT", [8, T], BF16)
    rmsnorm_T(k, A, psb, identb, lambda tau: I["x"][tau * 128:(tau + 1) * 128, :], g_premix, uT, "p1")
    if "uT" in DBG:
        k.dma("sp", DBG["uT"].rearrange("(k p) t -> p k t", p=128), uT)
    if upto <= 1:
        return
    Cs = {"identb": identb, "identf": identf}
    for n, sh, dt in (("onesb", [128], BF16), ("onesf", [128], F32), ("blocktri", [128], F32), ("MT", [128], F32),
                      ("US", [128], F32), ("nmask", [128], BF16)):
        Cs[n] = A.view(n, sh, dt)
        k.dma("sp", Cs[n], C[n])
    Cs["chunkind"] = A.view("chunkind", [2, 128], F32)
    k.dma("sp", Cs["chunkind"], C["chunkind"].rearrange("b p i -> p b i"))
    scr = nc.dram_tensor("scr_fm", [NT, 128, 48, 128], BF16).ap()
    obscr = nc.dram_tensor("scr_ob", [16, 128, T], BF16).ap()
    gdn_phase2(nc, k, A, psf, psb, I, Cs, uT, scr)
    if "scr" in DBG:
        k.dma("sp", DBG["scr"], scr)
    S = gdn_scalars(nc, k, A, psf, I, Cs, uT)
    for n in ("beta", "gc", "kds"):
        if n in DBG:
            k.dma("sp", DBG[n].rearrange("(t p) h -> p t h", p=128), S[n])
    if upto <= 2:
        return
    uscr = nc.dram_tensor("scr_uT", [128, 8, T], BF16).ap()
    k.dma("sp", uscr, uT)
    A.release("uT")
    gdn_main(nc, k, A, psf, psb, I, Cs, S, scr, obscr, DBG)
    uT = A.view("uT", [8, T], BF16)
    k.dma("sp", uT, uscr)
    if "obscr" in DBG:
        k.dma("sp", DBG["obscr"], obscr)
    if upto <= 3:
        return
    oaT = A.view("oaT", [8, T], BF16)
    da_phase(nc, k, A, psf, psb, I, Cs, uT, oaT)
    if "oaT" in DBG:
        k.dma("sp", DBG["oaT"].rearrange("(k p) t -> p k t", p=128), oaT)
    if upto <= 4:
        return
    for n in ("gs_beta", "gs_nbeta", "gs_gc", "gs_eg", "gs_kds", "gs_EGL"):
        A.release(n)
    tail_phases(nc, k, A, psf, psb, I, Cs, uT, oaT, obscr, out, DBG, upto)


def rmsnorm_T(k, A, psb, identb, src_fn, gT, dstT, tag, src_sbuf=False):
    ss = A.view(tag + "ss", [NT], F32)
    rs = A.view(tag + "rs", [NT], F32)
    junk = A.view(tag + "junk", [D], BF16)
    xb = [A.view(f"{tag}xb{i}", [D], F32) for i in range(2)] if not src_sbuf else None
    xn = [A.view(f"{tag}xn{i}", [D], BF16) for i in range(2)]
    for tau in range(NT):
        if src_sbuf:
            xt = src_fn(tau)
        else:
            xt = xb[tau % 2]
            k.dma("sp", xt, src_fn(tau))
        k.act(junk, xt, AF.Square, accum_out=ss[:, tau:tau + 1])
        k.act(rs[:, tau:tau + 1], ss[:, tau:tau + 1], AF.Ln, scale=1.0 / D, bias=EPS)
        k.act(rs[:, tau:tau + 1], rs[:, tau:tau + 1], AF.Exp, scale=-0.5)
        xnt = xn[tau % 2]
        k.ts("dve", xnt, xt, rs[:, tau:tau + 1], ALU.mult)
        for half in range(2):
            bank = psb[half + 0]
            for j in range(4):
                kk = half * 4 + j
                k.tr(bank[:, j * 128:(j + 1) * 128], xnt[:, kk * 128:(kk + 1) * 128], identb)
            src = bank[:, 0:512].rearrange("p (a b) -> p a b", b=128)
            k.tt("dve" if half == 0 else "dve", dstT[:, half * 4:half * 4 + 4, tau * 128:(tau + 1) * 128], src,
                 bcl(gT[:, half * 4:half * 4 + 4], 128), ALU.mult)
    for n in ("ss", "rs", "junk"):
        A.release(tag + n)
    for i in range(2):
        if not src_sbuf:
            A.release(f"{tag}xb{i}")
        A.release(f"{tag}xn{i}")


class Banks:
    def __init__(self, psf, psb, ids):
        self.psf, self.psb, self.ids, self.i = psf, psb, list(ids), 0

    def next(self):
        b = self.ids[self.i % len(self.ids)]
        self.i += 1
        return b


def wview(W, c0, n):
    return W[:, c0:c0 + n].rearrange("(kk p) c -> p kk c", p=128)


def gdn_phase2(nc, k, A, psf, psb, I, Cs, uT, scr):
    PB = Banks(psf, psb, [0, 1, 2, 3])
    OB = Banks(psf, psb, [4, 5, 6, 7])
    cw = A.view("cw", [32, 4], F32)
    k.dma("sp", cw, I["conv_w_T"])
    nh = A.view("p2nh", [512], F32)
    k.memset("pool", nh, -0.5)
    wb4 = [A.view(f"p2w{i}", [8, 512], BF16) for i in range(4)]
    seq = []
    for i in range(0, 32, 2):
        seq += [i, i + 1, 32 + i // 2]
    praw = [A.view(f"praw{i}", [3 + T + 1], F32) for i in range(2)]
    for i in range(2):
        k.memset("pool", praw[i][:, 0:3], 0.0)
    ycv = A.view("ycv", [T], F32)
    svb = [A.view(f"sv{i}", [T], F32) for i in range(4)]
    sqb = [A.view(f"sq{i}", [T], BF16) for i in range(4)]
    rsdb = [A.view(f"rsd{i}", [T], F32) for i in range(2)]
    res = [A.view(f"res{i}", [T], BF16) for i in range(6)]

    def loadw(g):
        buf = wb4[g % 2] if g < 8 else wb4[2 + g % 2]
        k.dma("pool", buf, wview(I["w_in"], 3072 + g * 512, 512))

    def store(c, r):
        k.dma("sp", scr[:, :, c, :].rearrange("t p j -> p t j"), r.rearrange("p (t j) -> p t j", j=128))

    def X(c):
        g = c // 4
        if c % 4 == 0 and g + 1 < 12 and g + 1 != 8:
            loadw(g + 1)
        wt = wb4[g % 2] if g < 8 else wb4[2 + g % 2]
        r = res[pos[c] % 6]
        pr = praw[c % 2]
        for t in range(4):
            b = PB.next()
            for kk in range(8):
                k.mm(psf[b], wt[:, kk, (c % 4) * 128:(c % 4 + 1) * 128], uT[:, kk, t * 512:(t + 1) * 512], start=(kk == 0), stop=(kk == 7))
            if c >= 32:
                k.act(r[:, t * 512:(t + 1) * 512], psf[b], AF.Silu)
            else:
                k.copy("act", pr[:, 3 + t * 512:3 + (t + 1) * 512], psf[b])
        if c >= 32:
            store(c, r)

    def Y1(c):
        if c >= 32:
            return
        pr = praw[c % 2]
        r = res[pos[c] % 6]
        k.ts("dve", ycv, pr[:, 0:T], cw[:, c, 0:1], ALU.mult)
        for j in range(1, 4):
            k.stt(ycv, pr[:, j:j + T], cw[:, c, j:j + 1], ycv, ALU.mult, ALU.add)
        if c >= 16:
            k.act(r, ycv, AF.Silu)
            store(c, r)
        else:
            k.act(svb[c % 4], ycv, AF.Silu)
            k.tt("pool", sqb[c % 4], svb[c % 4], svb[c % 4], ALU.mult)

    def Y2(c):
        if c < 0 or c >= 16:
            return
        sv, sq, r = svb[c % 4], sqb[c % 4], res[pos[c] % 6]
        rsd = rsdb[c % 2]
        for t in range(4):
            sl = slice(t * 512, (t + 1) * 512)
            b = OB.next()
            k.mm(psf[b], Cs["onesb"], sq[:, sl])
            qs = 128.0 if c < 8 else 1.0
            k.act(rsd[:, sl], psf[b], AF.Ln, bias=EPS * qs, scale=qs)
            k.act(rsd[:, sl], rsd[:, sl], AF.Exp, scale=-0.5)
        k.tt("pool", r, sv, rsd, ALU.mult)
        store(c, r)

    pos = {c: p for p, c in enumerate(seq)}
    loadw(0)
    loadw(8)
    X(seq[0])
    for p in range(48):
        if p + 1 < 48:
            X(seq[p + 1])
        if p % 2 == 0:
            for pp in (p - 3, p - 2):
                if pp >= 0:
                    Y2(seq[pp])
        Y1(seq[p])
    for n in (["cw", "p2nh", "ycv", "rsd0", "rsd1"] + [f"p2w{i}" for i in range(4)] + [f"praw{i}" for i in range(2)] + [f"res{i}" for i in range(6)]
              + [f"sv{i}" for i in range(4)] + [f"sq{i}" for i in range(4)]):
        A.release(n)


def gdn_scalars(nc, k, A, psf, I, Cs, uT):
    wba = A.view("wba", [8, 32], BF16)
    k.dma("pool", wba, wview(I["w_in"], 9216, 32))
    bank = psf[4]
    for tau in range(NT):
        for kk in range(8):
            k.mm(bank[:, tau * 32:(tau + 1) * 32], uT[:, kk, tau * 128:(tau + 1) * 128], wba[:, kk, :], start=(kk == 0), stop=(kk == 7))
    ba = A.view("ba", [16, 32], F32)
    k.copy("dve", ba, bank.rearrange("p (t c) -> p t c", c=32))
    S = {}
    for n in ("beta", "nbeta", "gc", "eg", "kds", "g", "spx", "glown"):
        S[n] = A.view("gs_" + n, [16, 16], F32)
    S["EGL"] = A.view("gs_EGL", [2, 16, 16], F32)
    GL = A.view("gs_GL", [2, 16, 16], F32)
    dtb = A.view("dtb", [16], F32)
    k.dma("sp", dtb, I["gdn_dt_bias"].unsqueeze(0).partition_broadcast(128)[:, 0, :])
    negA = A.view("negA", [16], F32)
    k.dma("sp", negA, I["gdn_a_log"].unsqueeze(0).partition_broadcast(128)[:, 0, :])
    k.act(negA, negA, AF.Exp)
    k.ts("dve", negA, negA, -1.0, ALU.mult)
    k.act(S["beta"], ba[:, :, 0:16], AF.Exp, scale=-1.0)
    k.ts("dve", S["beta"], S["beta"], 1.0, ALU.add)
    k.recip(S["beta"], S["beta"])
    k.ts("dve", S["nbeta"], S["beta"], -1.0, ALU.mult)
    k.tt("dve", S["spx"], ba[:, :, 16:32], dtb.unsqueeze(1).broadcast_to([128, 16, 16]), ALU.add)
    k.act(S["spx"], S["spx"], AF.Exp)
    k.act(S["spx"], S["spx"], AF.Ln, bias=1.0)
    k.tt("dve", S["g"], S["spx"], negA.unsqueeze(1).broadcast_to([128, 16, 16]), ALU.mult)
    g2 = S["g"].rearrange("p t h -> p (t h)")
    k.mm(psf[5][:, 0:256], Cs["blocktri"], g2)
    k.copy("dve", S["gc"], psf[5][:, 0:256].rearrange("p (t h) -> p t h", h=16))
    for b in range(2):
        k.mm(psf[6][:, b * 256:(b + 1) * 256], Cs["chunkind"][:, b, :], g2)
    k.copy("dve", GL, psf[6].rearrange("p (b t h) -> p b t h", b=2, h=16))
    k.act(S["EGL"], GL, AF.Exp)
    k.act(S["eg"], S["gc"], AF.Exp)
    k.copy("dve", S["glown"][0:64], GL[0:64, 0])
    k.copy("dve", S["glown"][64:128], GL[64:128, 1])
    k.tt("dve", S["kds"], S["glown"], S["gc"], ALU.subtract)
    k.act(S["kds"], S["kds"], AF.Exp)
    for n in ("wba", "ba", "dtb", "negA", "gs_GL", "gs_g", "gs_spx", "gs_glown"):
        A.release(n)
    return S


def gdn_main(nc, k, A, psf, psb, I, Cs, S, scr, obscr, dbg=None):
    B = Banks(psf, psb, range(8))
    identb, identf, onesb, onesf, MT, US = (Cs[n] for n in ("identb", "identf", "onesb", "onesf", "MT", "US"))
    gon = A.view("gon", [1], F32)
    k.dma("sp", gon, I["gdn_out_norm"].rearrange("(p o) -> p o", o=1))
    Sf = A.view("Sf", [16, 128], F32)
    Sb = A.view("Sb", [16, 128], BF16)
    k.memset("pool", Sf, 0.0)
    k.memset("pool", Sb, 0.0)
    qkvb = [A.view(f"fm{i}", [32, 128], BF16) for i in range(2)]
    szb = [A.view(f"sz{i}", [16, 128], BF16) for i in range(2)]
    DTb = [A.view(f"DT{i}", [16, 128], F32) for i in range(2)]
    DsBb = [A.view(f"DsB{i}", [16, 128], F32) for i in range(2)]
    DBUF = ("kdec", "Wb", "nwT", "intraT", "qgT")
    dbl = [{n: A.view(f"d{i}_{n}", [16, 128], BF16) for n in DBUF} for i in range(2)]
    upb = [A.view(f"d{i}_up", [16, 128], F32) for i in range(2)]
    vtmp = A.view("s_vtmp", [4, 128], F32)
    vnew = A.view("s_vnew", [16, 128], BF16)
    osb = A.view("s_osb", [16, 128], F32)
    sqo = A.view("s_sqo", [16, 128], F32)
    _o = A.used["s_sqo"][0]
    on = A.ap[:, _o // 2:_o // 2 + 2048].rearrange("p (a b) -> p a b", b=128)
    obt = A.view("s_obt", [16, 128], BF16)
    sso = A.view("s_sso", [16], F32)

    def g4(ap3):
        return ap3.rearrange("p (g r) i -> p g r i", r=2)

    def rep2(ap3):
        sh = list(ap3.shape)
        return ap3.unsqueeze(2).broadcast_to([sh[0], sh[1], 2, sh[2]])

    def f2(ap3):
        return ap3.rearrange("p h i -> p (h i)")

    def parA(tau):
        sc = {n: S[n][:, tau, :] for n in ("nbeta", "gc")}
        Rg = A.view("t_Rg", [16, 128], F32)
        MG = A.view("t_MG", [16, 128], F32)
        k.tt("pool", Rg, identf.unsqueeze(1).broadcast_to([128, 16, 128]), bcl(sc["gc"], 128), ALU.mult)
        k.tt("pool", MG, MT.unsqueeze(1).broadcast_to([128, 16, 128]), bcl(sc["gc"], 128), ALU.subtract)
        yield
        DT = DTb[tau % 2]
        for hg in range(4):
            b = B.next()
            k.mm(psf[b], onesf, f2(Rg[:, 4 * hg:4 * hg + 4, :]), start=True, stop=False)
            k.mm(psf[b], identf, f2(MG[:, 4 * hg:4 * hg + 4, :]), start=False, stop=True)
            k.act(f2(DT[:, 4 * hg:4 * hg + 4, :]), psf[b], AF.Exp)
        A.release("t_Rg")
        A.release("t_MG")
        yield
        DsB = DsBb[tau % 2]
        k.tt("pool", DsB, DT, US.unsqueeze(1).broadcast_to([128, 16, 128]), ALU.mult)
        k.tt("pool", DsB, DsB, bcl(sc["nbeta"], 128), ALU.mult)
        yield

    def par(tau):
        fm = qkvb[tau % 2]
        d = dbl[tau % 2]
        if tau + 1 < NT:
            k.dma("sp", qkvb[(tau + 1) % 2], scr[tau + 1][:, 0:32, :])
        k.dma("sp", szb[tau % 2], scr[tau][:, 32:48, :])
        qT = fm[:, 0:8, :]
        kT = fm[:, 8:16, :]
        vT = fm[:, 16:32, :]
        sc = {n: S[n][:, tau, :] for n in ("beta", "nbeta", "gc", "eg", "kds")}
        V = lambda n, sh, dt: A.view(f"t_{n}", sh, dt)
        ktm = V("ktm", [8, 128], BF16)
        b = B.next()
        for g in range(8):
            k.tr(psb[b][:, g * 128:(g + 1) * 128], kT[:, g, :], identb)
        k.copy("act", ktm, psb[b].rearrange("p (g i) -> p g i", i=128))
        vtm = V("vtm", [16, 128], BF16)
        for hh in range(2):
            b = B.next()
            for j in range(8):
                k.tr(psb[b][:, j * 128:(j + 1) * 128], vT[:, hh * 8 + j, :], identb)
            k.copy("act" if hh == 0 else "dve", vtm[:, hh * 8:hh * 8 + 8, :], psb[b].rearrange("p (g i) -> p g i", i=128))
        kg = V("kg", [16, 128], BF16)
        kdec = d["kdec"]
        yield
        DT = DTb[tau % 2]
        DsB = DsBb[tau % 2]
        Abuf = [V("A0", [16, 128], BF16), V("A1", [16, 128], BF16)]
        ATbuf = [V("AT0", [16, 128], BF16), V("AT1", [16, 128], BF16)]
        intraT = d["intraT"]
        for gb in range(2):
            b = B.next()
            for j in range(4):
                g = gb * 4 + j
                k.mm(psf[b][:, j * 128:(j + 1) * 128], kT[:, g, :], kT[:, g, :])
            k.tt("dve", g4(Abuf[0][:, 8 * gb:8 * gb + 8, :]), rep2(psf[b].rearrange("p (g i) -> p g i", i=128)),
                 g4(DsB[:, 8 * gb:8 * gb + 8, :]), ALU.mult)
            b = B.next()
            for j in range(4):
                g = gb * 4 + j
                k.mm(psf[b][:, j * 128:(j + 1) * 128], kT[:, g, :], qT[:, g, :])
            k.tt("dve", g4(intraT[:, 8 * gb:8 * gb + 8, :]), rep2(psf[b].rearrange("p (g i) -> p g i", i=128)),
                 g4(DT[:, 8 * gb:8 * gb + 8, :]), ALU.mult)
        yield
        Wb = d["Wb"]
        k.tt("pool", Wb, Abuf[0], identb.unsqueeze(1).broadcast_to([128, 16, 128]), ALU.add)
        for hh in range(2):
            b = B.next()
            for j in range(8):
                k.tr(psb[b][:, j * 128:(j + 1) * 128], Abuf[0][:, hh * 8 + j, :], identb)
            k.copy("act" if hh == 0 else "dve", ATbuf[0][:, hh * 8:hh * 8 + 8, :], psb[b].rearrange("p (g i) -> p g i", i=128))
        yield
        for l in range(1, 6):
            src, dst = (l - 1) % 2, l % 2
            for hg in range(4):
                hs = slice(4 * hg, 4 * hg + 4)
                bT = B.next()
                for j in range(4):
                    h = 4 * hg + j
                    k.mm(psf[bT][:, j * 128:(j + 1) * 128], Abuf[src][:, h, :], ATbuf[src][:, h, :])
                k.copy("act", f2(ATbuf[dst][:, hs, :]), psf[bT])
                if l < 5:
                    bA = B.next()
                    for j in range(4):
                        h = 4 * hg + j
                        k.mm(psf[bA][:, j * 128:(j + 1) * 128], ATbuf[src][:, h, :], Abuf[src][:, h, :])
                    k.copy("act" if hg % 2 == 0 else "dve", f2(Abuf[dst][:, hs, :]), psf[bA])
            yield
            for hg in range(4):
                hs = slice(4 * hg, 4 * hg + 4)
                bW = B.next()
                k.mm(psf[bW], identb, f2(Wb[:, hs, :]), start=True, stop=False)
                for j in range(4):
                    h = 4 * hg + j
                    k.mm(psf[bW][:, j * 128:(j + 1) * 128], ATbuf[dst][:, h, :], Wb[:, h, :], start=False, stop=(j == 3))
                k.copy("act" if hg % 2 == 0 else "dve", f2(Wb[:, hs, :]), psf[bW])
            yield
            if l == 1:
                k.tt("pool", g4(kg), rep2(ktm), g4(bcl(sc["eg"], 128)), ALU.mult)
                k.tt("pool", g4(kdec), rep2(ktm), g4(bcl(sc["kds"], 128)), ALU.mult)
                Re = V("Re", [16, 128], BF16)
                k.tt("dve", Re, identb.unsqueeze(1).broadcast_to([128, 16, 128]), bcl(sc["eg"], 128), ALU.mult)
                qgT = d["qgT"]
                for hg in range(4):
                    b = B.next()
                    k.mm(psf[b], onesb, f2(Re[:, 4 * hg:4 * hg + 4, :]))
                    k.tt("dve", g4(qgT[:, 4 * hg:4 * hg + 4, :]), psf[b].rearrange("p (g r i) -> p g r i", r=2, i=128),
                         rep2(qT[:, 2 * hg:2 * hg + 2, :]), ALU.mult)
                A.release("t_Re")
                yield
        for n in ("A0", "A1", "AT0", "AT1"):
            A.release("t_" + n)
        if tau + 1 < NT:
            for _ in parA(tau + 1):
                yield
        nwT = d["nwT"]
        for hg in range(4):
            b = B.next()
            for j in range(4):
                h = 4 * hg + j
                k.mm(psf[b][:, j * 128:(j + 1) * 128], kg[:, h, :], Wb[:, h, :])
            k.act(f2(nwT[:, 4 * hg:4 * hg + 4, :]), psf[b], AF.Copy, scale=-1.0)
        yield
        up = upb[tau % 2]
        for hg in range(4):
            b = B.next()
            for j in range(4):
                h = 4 * hg + j
                k.mm(psf[b][:, j * 128:(j + 1) * 128], Wb[:, h, :], vtm[:, h, :])
            k.copy("act", f2(up[:, 4 * hg:4 * hg + 4, :]), psf[b])
        A.release("t_ktm")
        A.release("t_kg")
        A.release("t_vtm")
        yield

    def seq(tau):
        d = dbl[tau % 2]
        up = upb[tau % 2]
        szT = szb[tau % 2]
        beta = S["beta"][:, tau, :]
        kdec, Wb, nwT, intraT, qgT = (d[n] for n in DBUF)
        for cb in range(2):
            r = slice(64 * cb, 64 * cb + 64)
            for hg in range(4):
                hs = slice(4 * hg, 4 * hg + 4)
                k.tt("pool", Sf[:, hs, :], Sf[:, hs, :], bcl(S["EGL"][:, cb, tau, hs], 128), ALU.mult)
                b1 = B.next()
                for j in range(4):
                    h = 4 * hg + j
                    k.mm(psf[b1][:, j * 128:(j + 1) * 128], nwT[:, h, :], Sb[:, h, :])
                bx = B.next()
                for j in range(4):
                    h = 4 * hg + j
                    k.mm(psf[bx][:, j * 128:(j + 1) * 128], qgT[:, h, :], Sb[:, h, :])
                k.tt("dve", vtmp[r], psf[b1][r, :].rearrange("p (h i) -> p h i", i=128), up[r, hs, :], ALU.add)
                k.tt("dve", vnew[r, hs, :], vtmp[r], bcl(beta[r, hs], 128), ALU.mult)
                k.copy("act", f2(osb[r, hs, :]), psf[bx][r, :])
            yield
            for hg in range(4):
                hs = slice(4 * hg, 4 * hg + 4)
                b3 = B.next()
                for j in range(4):
                    h = 4 * hg + j
                    k.mm(psf[b3][:, j * 128:(j + 1) * 128], kdec[r, h, :], vnew[r, h, :])
                by = B.next()
                for j in range(4):
                    h = 4 * hg + j
                    k.mm(psf[by][:, j * 128:(j + 1) * 128], intraT[r, h, :], vnew[r, h, :])
                sfs = Sf[:, hs, :]
                k.tt("dve", sfs, sfs, psf[b3].rearrange("p (h i) -> p h i", i=128), ALU.add)
                k.copy("act", Sb[:, hs, :], sfs)
                k.tt("dve", osb[r, hs, :], osb[r, hs, :], psf[by][r, :].rearrange("p (h i) -> p h i", i=128), ALU.add)
            yield
        k.tt("pool", sqo, osb, osb, ALU.mult)
        k.reduce(sso, sqo)
        k.act(sso, sso, AF.Ln, scale=1.0 / 128, bias=EPS)
        k.act(sso, sso, AF.Exp, scale=-0.5)
        k.tt("dve", on, osb, bcl(sso, 128), ALU.mult)
        if dbg is not None and "osb" in dbg:
            k.dma("sp", dbg["osb"][tau * 128:(tau + 1) * 128, :].rearrange("p (h i) -> p h i", i=128), osb)
        yield
        for hh in range(2):
            b = B.next()
            for j in range(8):
                k.tr(psb[b][:, j * 128:(j + 1) * 128], on[:, hh * 8 + j, :], identb)
            k.stt(f2(obt[:, hh * 8:hh * 8 + 8, :]), psb[b], gon[:, 0:1],
                  f2(szT[:, hh * 8:hh * 8 + 8, :]), ALU.mult, ALU.mult)
        k.dma("sp", obscr[:, :, tau * 128:(tau + 1) * 128].rearrange("h p i -> p h i"), obt)
        yield

    def drive(gp, gs, ratio):
        dp = ds = False
        while not (dp and ds):
            for _ in range(ratio):
                if not dp:
                    try:
                        next(gp)
                    except StopIteration:
                        dp = True
            if not ds:
                try:
                    next(gs)
                except StopIteration:
                    ds = True

    k.dma("sp", qkvb[0], scr[0][:, 0:32, :])
    for _ in parA(0):
        pass
    for _ in par(0):
        pass
    for tau in range(NT):
        gp = par(tau + 1) if tau + 1 < NT else iter(())
        drive(gp, seq(tau), 3)
    for n in ["gon", "Sf", "Sb", "fm0", "fm1", "sz0", "sz1", "DT0", "DT1", "DsB0", "DsB1", "s_vnew", "s_osb", "s_sqo", "s_obt", "s_sso", "s_vtmp", "d0_up", "d1_up"] + [f"d{i}_{n}" for i in range(2) for n in DBUF]:
        A.release(n)


def da_phase(nc, k, A, psf, psb, I, Cs, uT, oaT):
    identb, onesb, nmask = Cs["identb"], Cs["onesb"], Cs["nmask"]
    ACC = [(0, 1), (2, 3)]
    SB_ = Banks(psf, psb, [4, 5, 6, 7])
    MB = Banks(psf, psb, [7, 6, 5, 4])
    lv = A.view("lv", [4, 64], F32)
    for i, n in enumerate(("lambda_q1", "lambda_k1", "lambda_q2", "lambda_k2")):
        k.dma("sp", lv[:, i, :], I[n].unsqueeze(0).partition_broadcast(128)[:, 0, :])
    lp = A.view("lp", [2, 64], F32)
    k.tt("dve", lp[:, 0, :], lv[:, 0, :], lv[:, 1, :], ALU.mult)
    k.tt("dve", lp[:, 1, :], lv[:, 2, :], lv[:, 3, :], ALU.mult)
    ls = A.view("ls", [2], F32)
    k.reduce(ls, lp)
    k.act(ls, ls, AF.Exp)
    nlam = A.view("nlam", [1], F32)
    k.tt("dve", nlam, ls[:, 1:2], ls[:, 0:1], ALU.subtract)
    k.ts("dve", nlam, nlam, -0.2, ALU.add)
    gsub = A.view("gsub", [1], F32)
    k.dma("sp", gsub, I["da_sub_norm"].rearrange("(p o) -> p o", o=1))
    k.ts("dve", gsub, gsub, 0.8, ALU.mult)
    Vall = A.view("da_V", [NT, 1024], BF16)
    wv = [A.view(f"da_wv{i}", [8, 512], BF16) for i in range(2)]
    for half in range(2):
        k.dma("pool", wv[half], wview(I["w_in"], 2048 + half * 512, 512))
    for half in range(2):
        for tau in range(NT):
            b = SB_.next()
            for kk in range(8):
                k.mm(psf[b], uT[:, kk, tau * 128:(tau + 1) * 128], wv[half][:, kk, :], start=(kk == 0), stop=(kk == 7))
            k.copy("act" if tau % 2 == 0 else "dve", Vall[:, tau, half * 512:(half + 1) * 512], psf[b])
    wq = [A.view(f"daw{i}", [8, 128], BF16) for i in range(4)]
    qTb = [A.view(f"da_qT{i}", [T], BF16) for i in range(2)]
    kTb = [A.view(f"da_kT{i}", [T], BF16) for i in range(2)]
    oacc = A.view("da_oacc", [T], F32)
    pts = [A.view(f"da_pt{i}", [512], BF16) for i in range(6)]
    rz = A.view("da_rz", [512], F32)
    rz2 = A.view("da_rz2", [512], F32)
    tmpo = A.view("da_tmpo", [512], F32)
    sq = A.view("da_sq", [T], BF16)
    rsd = A.view("da_rsd", [512], F32)
    st = {"w": 0, "pt": 0}

    def proj(h):
        for which, dst, col0 in (("q", qTb[h % 2], h * 128), ("k", kTb[h % 2], 1024 + h * 128)):
            wt = wq[st["w"] % 4]
            st["w"] += 1
            k.dma("pool", wt, wview(I["w_in"], col0, 128))
            for t in range(4):
                b = MB.next()
                for kk in range(8):
                    k.mm(psf[b], wt[:, kk, :], uT[:, kk, t * 512:(t + 1) * 512], start=(kk == 0), stop=(kk == 7))
                if which == "q":
                    k.ts("dve", dst[:, t * 512:(t + 1) * 512], psf[b], 0.125, ALU.mult)
                else:
                    k.copy("dve", dst[:, t * 512:(t + 1) * 512], psf[b])

    def attn(h, qg):
        qT, kT = qTb[h % 2], kTb[h % 2]
        nk = 4 * qg + 4
        banks = {0: (0, 1), 1: (2, 3)}

        def scores(ki):
            q0 = max(ki * 128, qg * 512)
            nq = (qg + 1) * 512 - q0
            diag = ki >= 4 * qg
            items = []
            bs = [SB_.next(), SB_.next()]
            for j in range(2):
                pr = slice(64 * j, 64 * j + 64)
                k.mm(psf[bs[j]][:, 0:nq], kT[pr, ki * 128:(ki + 1) * 128], qT[pr, q0:q0 + nq], start=True, stop=not diag)
            for j in range(2):
                if diag:
                    k.mm(psf[bs[j]][:, 0:128], identb, nmask, start=False, stop=True)
                pt = pts[st["pt"] % 6]
                st["pt"] += 1
                k.act(pt[:, 0:nq], psf[bs[j]][:, 0:nq], AF.Exp)
                items.append((j, ki, q0 - qg * 512, nq, pt))
            return items

        def av(items):
            for j, ki, c0, nq, pt in items:
                bO, bZ = banks[j]
                k.mm(psf[bO][:, c0:c0 + nq], Vall[:, ki, h * 128:(h + 1) * 128], pt[:, 0:nq], start=(ki == 0), stop=(ki == nk - 1))
                k.mm(psf[bZ][:, c0:c0 + nq], onesb, pt[:, 0:nq], start=(ki == 0), stop=(ki == nk - 1))

        q = []
        for ki in range(nk):
            q.append(scores(ki))
            if len(q) > 1:
                av(q.pop(0))
        while q:
            av(q.pop(0))
        cols = slice(qg * 512, (qg + 1) * 512)
        k.act(rz, psf[1], AF.Ln)
        k.act(rz, rz, AF.Exp, scale=-1.0)
        k.tt("dve", oacc[:, cols], psf[0], rz, ALU.mult)
        k.act(rz2, psf[3], AF.Ln)
        k.act(rz2, rz2, AF.Exp, scale=-1.0)
        k.tt("dve", tmpo, psf[2], rz2, ALU.mult)
        k.stt(oacc[:, cols], tmpo, nlam[:, 0:1], oacc[:, cols], ALU.mult, ALU.add)

    def finalize(h):
        k.act(sq, oacc, AF.Square)
        for t in range(4):
            tsl = slice(t * 512, (t + 1) * 512)
            b = MB.next()
            k.mm(psf[b], onesb, sq[:, tsl])
            k.act(rsd, psf[b], AF.Ln, scale=1.0 / 128, bias=EPS)
            k.act(rsd, rsd, AF.Exp, scale=-0.5)
            k.stt(oaT[:, h, tsl], oacc[:, tsl], gsub[:, 0:1], rsd, ALU.mult, ALU.mult)

    proj(0)
    for h in range(8):
        for qg in range(4):
            attn(h, qg)
            if qg == 1 and h + 1 < 8:
                proj(h + 1)
        finalize(h)
    for n in (["lv", "lp", "ls", "nlam", "gsub", "da_V", "da_wv0", "da_wv1", "da_oacc", "da_rz", "da_rz2", "da_tmpo", "da_sq", "da_rsd"]
              + [f"daw{i}" for i in range(4)] + [f"da_pt{i}" for i in range(6)] + [f"da_qT{i}" for i in range(2)] + [f"da_kT{i}" for i in range(2)]):
        A.release(n)


def bcast_vec(k, A, name, vec):
    t = A.view(name, [int(vec.shape[0])], F32)
    k.dma("sp", t, vec.unsqueeze(0).partition_broadcast(128)[:, 0, :])
    return t


NHALF = [None]


def norm_res(k, A, src2, gain_b, resid, out_ap, tag):
    ss2 = A.view(tag + "ss2", [2], F32)
    junk = A.view(tag + "junk", [512], BF16)
    rs = A.view(tag + "rs", [1], F32)
    for h in range(2):
        k.act(junk, src2[h], AF.Square, accum_out=ss2[:, h:h + 1])
    k.tt("dve", rs, ss2[:, 0:1], ss2[:, 1:2], ALU.add)
    k.ts("dve", rs, rs, 1.0 / D, ALU.mult, EPS, ALU.add)
    k.tt("pool", rs, rs, NHALF[0], ALU.pow)
    for h in range(2):
        sl = slice(h * 512, (h + 1) * 512)
        k.stt(out_ap[:, sl], src2[h], rs[:, 0:1], gain_b[:, sl], ALU.mult, ALU.mult)
    k.tt("dve", out_ap, out_ap, resid, ALU.add)
    for n in ("ss2", "junk", "rs"):
        A.release(tag + n)


def tail_phases(nc, k, A, psf, psb, I, Cs, uT, oaT, obscr, out, DBG, upto):
    identb = Cs["identb"]
    B = Banks(psf, psb, range(8))
    nh = A.view("nhalf", [1], F32)
    k.memset("pool", nh, -0.5)
    NHALF[0] = nh
    obT = A.view("obT", [16, T], BF16)
    for h in range(16):
        k.dma("sp", obT[:, h, :], obscr[h])
    mergedT = A.view("mergedT", [8, T], BF16)
    wts = {}
    for n, nk in (("wa", 8), ("wb", 16), ("wga", 8), ("wgb", 8)):
        wts[n] = [A.view(f"{n}{i}", [nk, 128], BF16) for i in range(2)]
    sg = [A.view(f"sg{i}", [512], F32) for i in range(2)]
    m1 = A.view("m1", [512], F32)
    m2 = A.view("m2", [512], F32)

    def load_merge_w(j):
        k.dma("pool", wts["wga"][j % 2], wview(I["w_in"], 9248 + j * 128, 128))
        k.dma("pool", wts["wgb"][j % 2], wview(I["w_in"], 10272 + j * 128, 128))
        k.dma("pool", wts["wa"][j % 2], wview(I["w_branch_a"], j * 128, 128))
        k.dma("pool", wts["wb"][j % 2], wview(I["w_branch_b"], j * 128, 128))

    load_merge_w(0)
    for j in range(8):
        if j + 1 < 8:
            load_merge_w(j + 1)
        wa, wb, wga, wgb = (wts[n][j % 2] for n in ("wa", "wb", "wga", "wgb"))
        for t in range(4):
            tsl = slice(t * 512, (t + 1) * 512)
            bA, bB, bGa, bGb = B.next(), B.next(), B.next(), B.next()
            for kk in range(8):
                k.mm(psf[bGa], wga[:, kk, :], uT[:, kk, tsl], start=(kk == 0), stop=(kk == 7))
            for kk in range(8):
                k.mm(psf[bGb], wgb[:, kk, :], uT[:, kk, tsl], start=(kk == 0), stop=(kk == 7))
            for kk in range(8):
                k.mm(psf[bA], wa[:, kk, :], oaT[:, kk, tsl], start=(kk == 0), stop=(kk == 7))
            for kk in range(16):
                k.mm(psf[bB], wb[:, kk, :], obT[:, kk, tsl], start=(kk == 0), stop=(kk == 15))
            k.act(sg[0], psf[bGa], AF.Sigmoid)
            k.act(sg[1], psf[bGb], AF.Sigmoid)
            k.tt("dve", m1, psf[bA], sg[0], ALU.mult)
            k.tt("dve", m2, psf[bB], sg[1], ALU.mult)
            k.tt("pool", mergedT[:, j, tsl], m1, m2, ALU.add)
    for n in ["obT", "m1", "m2", "sg0", "sg1"] + [f"{n}{i}" for n in ("wa", "wb", "wga", "wgb") for i in range(2)]:
        A.release(n)
    A.release("uT")
    A.release("oaT")
    if "mergedT" in DBG:
        k.dma("sp", DBG["mergedT"].rearrange("(k p) t -> p k t", p=128), mergedT)
    wout = A.view("wout", [8, D], BF16)
    for kk in range(8):
        k.dma("pool", wout[:, kk, :], I["w_out"][kk * 128:(kk + 1) * 128, :])
    wup = A.view("wup", [8, 4096], BF16)
    for kk in range(8):
        k.dma("pool", wup[:, kk, :], I["w_up"][kk * 128:(kk + 1) * 128, :])
    g_postmix = bcast_vec(k, A, "g_postmix", I["post_mix_norm"])
    g_premlp = A.view("g_premlp", [8], F32)
    k.dma("sp", g_premlp, I["pre_mlp_norm_T"])
    u2T = A.view("u2T", [8, T], BF16)
    h1b = [A.view(f"h1b{i}", [D], F32) for i in range(4)]
    xb = [A.view(f"xb{i}", [D], F32) for i in range(3)]
    ss6 = A.view("q6ss", [NT], F32)
    rs6 = A.view("q6rs", [NT], F32)
    junk6 = A.view("q6junk", [D], BF16)
    xn6 = [A.view(f"q6xn{i}", [D], BF16) for i in range(2)]

    def p6a(tau):
        ht, xt = h1b[tau % 4], xb[tau % 3]
        k.dma("sp", xt, I["x"][tau * 128:(tau + 1) * 128, :])
        bs = [B.next(), B.next()]
        for half in range(2):
            for kk in range(8):
                k.mm(psf[bs[half]], mergedT[:, kk, tau * 128:(tau + 1) * 128], wout[:, kk, half * 512:(half + 1) * 512],
                     start=(kk == 0), stop=(kk == 7))

        def fin():
            norm_res(k, A, [psf[bs[0]], psf[bs[1]]], g_postmix, xt, ht, "p6")
            k.dma("pool", out[tau * 128:(tau + 1) * 128, :], ht)
        return fin

    def p6b(tau):
        xt = h1b[tau % 4]
        k.act(junk6, xt, AF.Square, accum_out=ss6[:, tau:tau + 1])
        k.act(rs6[:, tau:tau + 1], ss6[:, tau:tau + 1], AF.Ln, scale=1.0 / D, bias=EPS)
        k.act(rs6[:, tau:tau + 1], rs6[:, tau:tau + 1], AF.Exp, scale=-0.5)
        xnt = xn6[tau % 2]
        k.ts("dve", xnt, xt, rs6[:, tau:tau + 1], ALU.mult)
        for half in range(2):
            b = B.next()
            for j in range(4):
                kk = half * 4 + j
                k.tr(psb[b][:, j * 128:(j + 1) * 128], xnt[:, kk * 128:(kk + 1) * 128], identb)
            src = psb[b][:, 0:512].rearrange("p (a b) -> p a b", b=128)
            k.tt("dve", u2T[:, half * 4:half * 4 + 4, tau * 128:(tau + 1) * 128], src,
                 bcl(g_premlp[:, half * 4:half * 4 + 4], 128), ALU.mult)

    p6a(0)()
    p6a(1)()
    for tau in range(NT):
        fin = p6a(tau + 2) if tau + 2 < NT else None
        p6b(tau)
        if fin is not None:
            fin()
    for n in ["mergedT", "wout", "g_postmix", "g_premlp", "h1b0", "h1b1", "h1b2", "h1b3", "xb0", "xb1", "xb2", "q6ss", "q6rs", "q6junk", "q6xn0", "q6xn1"]:
        A.release(n)
    if upto <= 6:
        return
    wdn = A.view("wdn", [32, D], BF16)
    for f in range(32):
        k.dma("pool", wdn[:, f, :], I["w_down"][f * 128:(f + 1) * 128, :])
    hscr = nc.dram_tensor("scr_hid", [NT, 128, 32, 128], BF16).ap()
    rl = [A.view(f"rl{i}", [512], F32) for i in range(2)]
    hd = [A.view(f"hd{i}", [T], BF16) for i in range(2)]
    for f in range(32):
        for t in range(4):
            b = B.next()
            for kk in range(8):
                k.mm(psf[b], wup[:, kk, f * 128:(f + 1) * 128], u2T[:, kk, t * 512:(t + 1) * 512], start=(kk == 0), stop=(kk == 7))
            r = rl[t % 2]
            k.act(r, psf[b], AF.Relu)
            k.tt("dve" if t % 2 == 0 else "pool", hd[f % 2][:, t * 512:(t + 1) * 512], r, r, ALU.mult)
        k.dma("sp", hscr[:, :, f, :].rearrange("t p j -> p t j"), hd[f % 2].rearrange("p (t j) -> p t j", j=128))
    for n in ["u2T", "wup"] + [f"rl{i}" for i in range(2)] + [f"hd{i}" for i in range(2)]:
        A.release(n)
    wple = A.view("wple", [2, D], BF16)
    for kk in range(2):
        k.dma("pool", wple[:, kk, :], I["w_ple"][kk * 128:(kk + 1) * 128, :])
    wpg = A.view("wpg", [8, D], BF16)
    for kk in range(8):
        k.dma("pool", wpg[:, kk, :], I["w_ple_gate"][kk * 128:(kk + 1) * 128, :])
    g_postmlp = bcast_vec(k, A, "g_postmlp", I["post_mlp_norm"])
    g_ple = bcast_vec(k, A, "g_ple", I["ple_norm"])
    hidb = [A.view(f"hidb{i}", [32, 128], BF16) for i in range(2)]
    h1t = [A.view(f"h1t{i}", [D], F32) for i in range(2)]
    h2t = [A.view(f"h2t{i}", [D], F32) for i in range(2)]
    h2bb = [A.view(f"h2b{i}", [D], BF16) for i in range(2)]
    h2T = A.view("h2T", [8, 128], BF16)
    pt32 = [A.view(f"pt32{i}", [256], F32) for i in range(2)]
    ptbb = [A.view(f"ptb{i}", [256], BF16) for i in range(2)]
    pT = A.view("pT", [2, 128], BF16)
    sgp = A.view("sgp", [D], F32)
    et = A.view("et", [D], F32)
    ot = [A.view(f"ot{i}", [D], F32) for i in range(2)]

    def p7d(tau):
        rows = slice(tau * 128, (tau + 1) * 128)
        hb = hidb[tau % 2]
        k.dma("sp", hb, hscr[tau])
        k.dma("sp", h1t[tau % 2], out[rows, :])
        k.dma("sp", pt32[tau % 2], I["p"][rows, :])
        bs = [0, 1] if tau % 2 == 0 else [2, 3]
        for half in range(2):
            for f in range(32):
                k.mm(psf[bs[half]], hb[:, f, :], wdn[:, f, half * 512:(half + 1) * 512], start=(f == 0), stop=(f == 31))
        h2 = h2t[tau % 2]

        def fin():
            norm_res(k, A, [psf[bs[0]], psf[bs[1]]], g_postmlp, h1t[tau % 2], h2, "p7")
            k.copy("act", h2bb[tau % 2], h2)
            k.copy("act", ptbb[tau % 2], pt32[tau % 2])
        return fin

    def p8e(tau):
        rows = slice(tau * 128, (tau + 1) * 128)
        h2, h2b, ptb = h2t[tau % 2], h2bb[tau % 2], ptbb[tau % 2]
        b = 4
        for kk in range(8):
            k.tr(psb[b][:, kk * 128:(kk + 1) * 128], h2b[:, kk * 128:(kk + 1) * 128], identb)
        k.copy("dve", h2T, psb[b].rearrange("p (a i) -> p a i", i=128))
        b = 5
        for kk in range(2):
            k.tr(psb[b][:, kk * 128:(kk + 1) * 128], ptb[:, kk * 128:(kk + 1) * 128], identb)
        k.copy("dve", pT, psb[b][:, 0:256].rearrange("p (a i) -> p a i", i=128))
        be = [6, 6]
        bg = [7, 7]
        for half in range(2):
            hs = slice(half * 512, (half + 1) * 512)
            for kk in range(8):
                k.mm(psf[bg[half]], h2T[:, kk, :], wpg[:, kk, hs], start=(kk == 0), stop=(kk == 7))
            for kk in range(2):
                k.mm(psf[be[half]], pT[:, kk, :], wple[:, kk, hs], start=(kk == 0), stop=(kk == 1))
            k.act(sgp[:, hs], psf[bg[half]], AF.Sigmoid)
            k.tt("dve", et[:, hs], psf[be[half]], sgp[:, hs], ALU.mult)
        o = ot[tau % 2]
        norm_res(k, A, [et[:, 0:512], et[:, 512:1024]], g_ple, h2, o, "p8")
        k.dma("pool", out[rows, :], o)

    p7d(0)()
    for tau in range(NT):
        fin = p7d(tau + 1) if tau + 1 < NT else None
        p8e(tau)
        if fin is not None:
            fin()


_NC_CACHE = {}


def kernel(**inputs):
    n = 8
    if "nc" not in _NC_CACHE:
        _NC_CACHE["nc"] = build()
    nc = _NC_CACHE["nc"]
    consts = host_consts()
    shared = {}
    for name, sh in IN_SHAPES.items():
        if name in ("x", "p"):
            continue
        shared[name] = np.ascontiguousarray(np.asarray(inputs[name], dtype=np.float32)[0].reshape(sh))
    shared.update(host_layout(shared))
    shared.update(consts)
    x = np.asarray(inputs["x"], dtype=np.float32)
    p = np.asarray(inputs["p"], dtype=np.float32)
    in_maps = []
    for b in range(n):
        m = dict(shared)
        m["x"] = np.ascontiguousarray(x[b])
        m["p"] = np.ascontiguousarray(p[0, b])
        in_maps.append(m)
    res = run_bass_kernel_spmd(nc, in_maps, core_ids=list(range(n)))
    return np.stack([np.asarray(r["out"], dtype=np.float32) for r in res.results], axis=0)
```
